# Optimizing a Trainium2 kernel written in Bass

```python
import jax, jax.numpy as jnp
from jax import lax
import numpy as np

D_MODEL = 1024
BATCH = 8
SEQ = 2048
DEPTH = 2
DEC_BATCH = 128
DEC_SEQ = 4
PAST_LEN = 16384
PAGE_SIZE = 128

N_MEM = 256
ATT_HEADS = 4
ATT_HEAD_DIM = D_MODEL // ATT_HEADS
ATT_WIDTH = ATT_HEADS * ATT_HEAD_DIM
W_CONV = D_MODEL
CONV_WIDTH = 3
W_POOL = D_MODEL
POOL_WINDOWS = (2, 4, 8, 16)
N_POOL_GROUPS = 4
POOL_GROUP = W_POOL // N_POOL_GROUPS
POOL_STATE = 15
EPS = 1e-6
IN_SPLITS = (W_CONV, W_CONV, W_CONV, W_CONV, W_POOL, W_POOL, ATT_WIDTH, ATT_WIDTH, D_MODEL, D_MODEL, D_MODEL)
D_IN = sum(IN_SPLITS)

kernel_name = "hybrid_conv_pool_memxattn_decoder_step"


def rmsnorm(x, g):
    xf = x.astype(jnp.float32)
    y = xf * lax.rsqrt(jnp.mean(xf * xf, axis=-1, keepdims=True) + EPS)
    return (y * g.astype(jnp.float32)).astype(x.dtype)


def mem_keys_values(mem, g, w_kv):
    kv = rmsnorm(mem, g) @ w_kv
    k, v = jnp.split(kv, 2, axis=-1)
    b = mem.shape[0]
    return (k.reshape(b, N_MEM, ATT_HEADS, ATT_HEAD_DIM),
            v.reshape(b, N_MEM, ATT_HEADS, ATT_HEAD_DIM))


def causal_multiscale_pool(p_ext, pos):
    b, l, _ = p_ext.shape
    t = l - POOL_STATE
    pf = p_ext.astype(jnp.float32).reshape(b, l, N_POOL_GROUPS, POOL_GROUP)
    cs = jnp.concatenate([jnp.zeros_like(pf[:, :1]), jnp.cumsum(pf, axis=1)], axis=1)
    end = cs[:, POOL_STATE + 1:]
    means = []
    for g, w in enumerate(POOL_WINDOWS):
        start = cs[:, POOL_STATE + 1 - w: POOL_STATE + 1 - w + t, g]
        cnt = jnp.minimum(pos + 1, w).astype(jnp.float32)[None, :, None]
        means.append((end[:, :, g] - start) / cnt)
    mean = jnp.stack(means, axis=2)
    return mean - pf[:, POOL_STATE:]


def layer(x, mem_k, mem_v, conv_prev, pool_prev, pos,
          norm_g, w_in, conv_w, pool_w, pool_scale, w_br_conv, w_br_pool, w_br_att, w_out):
    b, t, _ = x.shape
    h = rmsnorm(x, norm_g)
    z = h @ w_in
    idx = np.cumsum(IN_SPLITS)[:-1].tolist()
    hc, bc, cc, gc, hp, gp, q, ga, mc, mp, ma = jnp.split(z, idx, axis=-1)

    u = cc * hc
    u_ext = jnp.concatenate([conv_prev.astype(u.dtype), u], axis=1)
    y_conv = conv_w[0] * u_ext[:, 0:t]
    for k in range(1, CONV_WIDTH):
        y_conv = y_conv + conv_w[k] * u_ext[:, k:k + t]
    conv_br = (bc * y_conv * jax.nn.silu(gc)) @ w_br_conv

    p_ext = jnp.concatenate([pool_prev.astype(hp.dtype), hp], axis=1)
    mixed = causal_multiscale_pool(p_ext, pos)
    pooled = jnp.einsum('btgc,gcd->btgd', mixed, pool_w.astype(jnp.float32))
    pooled = pooled.reshape(b, t, W_POOL).astype(x.dtype) * pool_scale
    pool_br = (pooled * jax.nn.silu(gp)) @ w_br_pool

    qh = q.reshape(b, t, ATT_HEADS, ATT_HEAD_DIM)
    s = jnp.einsum('bthd,bmhd->bhtm', qh, mem_k).astype(jnp.float32) * (ATT_HEAD_DIM ** -0.5)
    p = jax.nn.softmax(s, axis=-1).astype(x.dtype)
    o = jnp.einsum('bhtm,bmhd->bthd', p, mem_v).reshape(b, t, ATT_WIDTH)
    att_br = (o * jax.nn.silu(ga)) @ w_br_att

    merged = (jax.nn.sigmoid(mc) * conv_br + jax.nn.sigmoid(mp) * pool_br
              + jax.nn.sigmoid(ma) * att_br)
    x_new = x + merged @ w_out
    return x_new, u_ext[:, -(CONV_WIDTH - 1):], p_ext[:, -POOL_STATE:]


def setup_inputs(seed: int = 0) -> dict:
    key = jax.random.key(seed)
    ks = jax.random.split(key, 24)
    f32 = jnp.float32
    nrm = lambda k, shape, scale: jax.random.normal(k, shape, f32) * scale
    return {
        "x_prompt": nrm(ks[0], (BATCH, SEQ, D_MODEL), 1.0),
        "x_sample": nrm(ks[1], (DEC_BATCH, DEC_SEQ, D_MODEL), 1.0),
        "mem_prompt": nrm(ks[2], (BATCH, N_MEM, D_MODEL), 1.0),
        "cache_mem_k": nrm(ks[3], (DEPTH, DEC_BATCH, N_MEM, ATT_HEADS, ATT_HEAD_DIM), 1.0),
        "cache_mem_v": nrm(ks[4], (DEPTH, DEC_BATCH, N_MEM, ATT_HEADS, ATT_HEAD_DIM), 1.0),
        "state_conv": nrm(ks[5], (DEPTH, DEC_BATCH, CONV_WIDTH - 1, W_CONV), 1.0),
        "state_pool": nrm(ks[6], (DEPTH, DEC_BATCH, POOL_STATE, W_POOL), 1.0),
        "norm_g": 1.0 + nrm(ks[7], (DEPTH, D_MODEL), 0.05),
        "w_in": nrm(ks[8], (DEPTH, D_MODEL, D_IN), D_MODEL ** -0.5),
        "conv_w": nrm(ks[9], (DEPTH, CONV_WIDTH, W_CONV), CONV_WIDTH ** -0.5),
        "pool_w": nrm(ks[10], (DEPTH, N_POOL_GROUPS, POOL_GROUP, POOL_GROUP), POOL_GROUP ** -0.5),
        "pool_scale": 1.0 + nrm(ks[11], (DEPTH, W_POOL), 0.1),
        "mem_norm_g": 1.0 + nrm(ks[12], (DEPTH, D_MODEL), 0.05),
        "w_mem_kv": nrm(ks[13], (DEPTH, D_MODEL, 2 * ATT_WIDTH), D_MODEL ** -0.5),
        "w_br_conv": nrm(ks[14], (DEPTH, W_CONV, D_MODEL), W_CONV ** -0.5),
        "w_br_pool": nrm(ks[15], (DEPTH, W_POOL, D_MODEL), W_POOL ** -0.5),
        "w_br_att": nrm(ks[16], (DEPTH, ATT_WIDTH, D_MODEL), ATT_WIDTH ** -0.5),
        "w_out": nrm(ks[17], (DEPTH, D_MODEL, D_MODEL), D_MODEL ** -0.5),
        "final_norm_g": 1.0 + nrm(ks[18], (D_MODEL,), 0.05),
    }


def reference(x_prompt, x_sample, mem_prompt, cache_mem_k, cache_mem_v, state_conv, state_pool,
              norm_g, w_in, conv_w, pool_w, pool_scale, mem_norm_g, w_mem_kv,
              w_br_conv, w_br_pool, w_br_att, w_out, final_norm_g):
    b_p, t_p, _ = x_prompt.shape
    t_s = x_sample.shape[1]
    pos_p = jnp.arange(t_p, dtype=jnp.int32)
    pos_s = PAST_LEN + jnp.arange(t_s, dtype=jnp.int32)
    xp, xs = x_prompt, x_sample
    mk_p, mv_p, cv_p, pl_p, cv_s, pl_s = [], [], [], [], [], []
    for l in range(DEPTH):
        lw = (norm_g[l], w_in[l], conv_w[l], pool_w[l], pool_scale[l],
              w_br_conv[l], w_br_pool[l], w_br_att[l], w_out[l])
        k_p, v_p = mem_keys_values(mem_prompt, mem_norm_g[l], w_mem_kv[l])
        conv0 = jnp.zeros((b_p, CONV_WIDTH - 1, W_CONV), xp.dtype)
        pool0 = jnp.zeros((b_p, POOL_STATE, W_POOL), xp.dtype)
        xp, c_new, p_new = layer(xp, k_p, v_p, conv0, pool0, pos_p, *lw)
        mk_p.append(k_p); mv_p.append(v_p); cv_p.append(c_new); pl_p.append(p_new)
        xs, c_new_s, p_new_s = layer(xs, cache_mem_k[l], cache_mem_v[l], state_conv[l], state_pool[l],
                                     pos_s, *lw)
        cv_s.append(c_new_s); pl_s.append(p_new_s)
    y_prompt = rmsnorm(xp, final_norm_g)
    y_sample = rmsnorm(xs, final_norm_g)
    return (y_prompt, y_sample, jnp.stack(mk_p), jnp.stack(mv_p), jnp.stack(cv_p), jnp.stack(pl_p),
            jnp.stack(cv_s), jnp.stack(pl_s))
```

```python
import numpy as np
from contextlib import ExitStack
import concourse.bass as bass
import concourse.mybir as mybir
from concourse.bass_utils import run_bass_kernel_spmd

F32 = mybir.dt.float32
BF16 = mybir.dt.bfloat16
AF = mybir.ActivationFunctionType
ALU = mybir.AluOpType

D = 1024
KC = 8
SEQ = 2048
NMEM = 256
DIN = 11264
EPS = 1e-6
NT = 1088
RING = 7
NSLOT = 15
SLOTW = 528
WIN = (2, 4, 8, 16)
STOP_AFTER = None

OFF_HC, OFF_BC, OFF_CC, OFF_GC, OFF_HP, OFF_GP, OFF_Q, OFF_GA, OFF_MC, OFF_MP, OFF_MA = [i * 1024 for i in range(11)]

V_NG, V_CW, V_PS, V_MG, V_FG, V_RC = 0, 16, 64, 80, 96, 104
NV = 104 + 60


class _Op:
    __slots__ = ("eng", "fn", "deps", "inc", "val", "dma_key", "dma_val", "name")

    def __init__(self, eng, fn, dma_key, name):
        self.eng = eng
        self.fn = fn
        self.deps = []
        self.inc = False
        self.val = None
        self.dma_key = dma_key
        self.dma_val = None
        self.name = name


class _Seg:
    __slots__ = ("lo", "hi", "writers", "readers", "prev_readers")

    def __init__(self, lo, hi, writers=None, readers=None, prev_readers=None):
        self.lo, self.hi = lo, hi
        self.writers = list(writers) if writers else []
        self.readers = list(readers) if readers else []
        self.prev_readers = list(prev_readers) if prev_readers else []

    def split(self, lo, hi):
        return _Seg(lo, hi, self.writers, self.readers, self.prev_readers)


BIG = 1 << 30


class Tracker:
    ENGS = ("pe", "act", "dve", "pool", "sp")

    def __init__(self):
        self.streams = {e: [] for e in self.ENGS}
        self.segs = {}
        self.dma_cnt = {}

    def _touch(self, res):
        if isinstance(res, str):
            res = (res, 0, BIG)
        name, lo, hi = res
        assert lo < hi, res
        segs = self.segs.setdefault(name, [])
        new, out = [], []
        pos = lo
        for s in segs:
            if s.hi <= lo or s.lo >= hi:
                new.append(s)
                continue
            if s.lo < lo:
                new.append(s.split(s.lo, lo))
                s.lo = lo
            right = None
            if s.hi > hi:
                right = s.split(hi, s.hi)
                s.hi = hi
            if pos < s.lo:
                g = _Seg(pos, s.lo)
                new.append(g)
                out.append(g)
            new.append(s)
            out.append(s)
            pos = s.hi
            if right is not None:
                new.append(right)
        if pos < hi:
            g = _Seg(pos, hi)
            new.append(g)
            out.append(g)
        new.sort(key=lambda s: s.lo)
        self.segs[name] = new
        return out

    def op(self, eng, fn, reads=(), writes=(), extra=(), dma_key=None, name=""):
        o = _Op(eng, fn, dma_key, name)
        deps = {}

        def add(p, kind):
            if p is o or p is None:
                return
            if p.dma_key is None and p.eng == eng:
                if eng not in ("act", "dve", "pool"):
                    return
            deps[id(p)] = p

        reads, writes = list(reads), list(writes)
        bank_names = {r[0] for r in reads + writes if not isinstance(r, str) and r[0][0] == "B" and r[0][1:].isdigit()}
        if bank_names:
            reads = [r for r in reads if isinstance(r, str) or r[0] not in bank_names]
            writes = [r for r in writes if isinstance(r, str) or r[0] not in bank_names]
            for bn in bank_names:
                reads.append((bn, 0, BIG))
                writes.append((bn, 0, BIG))
        rsegs = [s for r in reads for s in self._touch(r)]
        wsegs = [s for r in writes for s in self._touch(r)]
        rsegs = [s for r in reads for s in self._touch(r)]
        for st in rsegs:
            for w in st.writers:
                add(w, "raw")
        for st in wsegs:
            if st.readers:
                st.prev_readers = st.readers
                st.readers = []
                st.writers = []
            for x in st.prev_readers:
                add(x, "war")
            for x in st.writers:
                add(x, "waw")
        for st in rsegs:
            st.readers.append(o)
        for st in wsegs:
            st.writers.append(o)
        for p in extra:
            add(p, "raw")
        for p in deps.values():
            if p.dma_key is None:
                p.inc = True
        o.deps = list(deps.values())
        if dma_key is not None:
            c = self.dma_cnt.get(dma_key, 0) + 1
            self.dma_cnt[dma_key] = c
            o.dma_val = 16 * c
        self.streams[eng].append(o)
        return o

    def finalize(self):
        for e in self.ENGS:
            c = 0
            for o in self.streams[e]:
                if o.dma_key is None and o.inc:
                    c += 1
                    o.val = c

    def emit(self, eng_name, eng, sems, dma_sems):
        waited = {}
        for o in self.streams[eng_name]:
            need = {}
            for p in o.deps:
                if p.dma_key is not None:
                    key, val, sem = ("d", p.dma_key), p.dma_val, dma_sems[p.dma_key]
                else:
                    key, val, sem = ("e", p.eng), p.val, sems[p.eng]
                if key not in need or need[key][0] < val:
                    need[key] = (val, sem)
            for key, (val, sem) in need.items():
                if waited.get(key, 0) >= val:
                    continue
                waited[key] = val
                eng.wait_ge(sem, val)
            ins = o.fn(eng)
            if o.dma_key is not None:
                ins.then_inc(dma_sems[o.dma_key], 16)
            elif o.inc:
                ins.then_inc(sems[o.eng], 1)


class Tile:
    def __init__(self, name, kind, col, n, su, pos0, half, row0):
        self.name, self.kind, self.col, self.n, self.su, self.pos0, self.half, self.row0 = (
            name, kind, col, n, su, pos0, half, row0)
        self.tl = n // su


HALVES = [
    [Tile("P0", "P", 0, 512, 1, 0, 0, 0), Tile("P1", "P", 512, 512, 1, 512, 0, 512), Tile("S", "S", 1024, 64, 16, None, 0, 0)],
    [Tile("P2", "P", 0, 512, 1, 1024, 1, 1024), Tile("P3", "P", 512, 512, 1, 1536, 1, 1536)],
]


def build_program(order=None):
    dry = order is None
    nc = bass.Bass("TRN2", target_bir_lowering=False)
    dt = nc.dram_tensor
    x_p = dt("x_p", [SEQ, D], F32, kind="ExternalInput").ap()
    x_s = dt("x_s", [64, D], F32, kind="ExternalInput").ap()
    mem = dt("mem", [NMEM, D], F32, kind="ExternalInput").ap()
    ck = dt("ck", [2, 16, NMEM, 4, 256], F32, kind="ExternalInput").ap()
    cv = dt("cv", [2, 16, NMEM, 4, 256], F32, kind="ExternalInput").ap()
    st_c = dt("st_c", [2, 32, D], F32, kind="ExternalInput").ap()
    st_p = dt("st_p", [2, 240, D], F32, kind="ExternalInput").ap()
    w_in = dt("w_in", [2, D, DIN], F32, kind="ExternalInput").ap()
    pool_w = dt("pool_w", [2, 4, 256, 256], F32, kind="ExternalInput").ap()
    w_kv = dt("w_kv", [2, D, 2048], F32, kind="ExternalInput").ap()
    w_bc = dt("w_bc", [2, D, D], F32, kind="ExternalInput").ap()
    w_bp = dt("w_bp", [2, D, D], F32, kind="ExternalInput").ap()
    w_ba = dt("w_ba", [2, D, D], F32, kind="ExternalInput").ap()
    w_o = dt("w_o", [2, D, D], F32, kind="ExternalInput").ap()
    vecs = dt("vecs", [128, NV], F32, kind="ExternalInput").ap()
    ident_in = dt("ident", [128, 128], F32, kind="ExternalInput").ap()

    y_p = dt("y_p", [SEQ, D], F32, kind="ExternalOutput").ap()
    y_s = dt("y_s", [64, D], F32, kind="ExternalOutput").ap()
    o_mk = dt("o_mk", [2, NMEM, D], F32, kind="ExternalOutput").ap()
    o_mv = dt("o_mv", [2, NMEM, D], F32, kind="ExternalOutput").ap()
    o_cvp = dt("o_cvp", [2, 2, D], F32, kind="ExternalOutput").ap()
    o_plp = dt("o_plp", [2, 15, D], F32, kind="ExternalOutput").ap()
    o_cvs = dt("o_cvs", [2, 32, D], F32, kind="ExternalOutput").ap()
    o_pls = dt("o_pls", [2, 240, D], F32, kind="ExternalOutput").ap()

    T = Tracker()
    es = ExitStack()
    with es:
        sb = lambda name, shape, dtype: es.enter_context(nc.sbuf_tensor(name, shape, dtype))
        xT = sb("xT", [128, KC, NT], F32)
        hT = sb("hT", [128, KC, NT], BF16)
        aC = sb("aC", [128, KC, NT], BF16)
        aP = sb("aP", [128, KC, NT], BF16)
        aA = sb("aA", [128, KC, NT], BF16)
        mgR = sb("mgR", [128, KC * NT], BF16)
        wring = sb("wring", [128, RING, KC, 256], BF16)
        KTp = sb("KTp", [128, 2, KC, NMEM], BF16)
        Vp = sb("Vp", [128, 2, 2, D], BF16)
        scr = sb("scr", [128, NSLOT * SLOTW], F32)
        slots = [scr[:, i * SLOTW:(i + 1) * SLOTW] for i in range(NSLOT)]
        vt = sb("vt", [128, NV], F32)
        ident = sb("identf", [128, 128], F32)
        identb = sb("identb", [128, 128], BF16)
        ones_m = sb("ones_m", [128, 128], BF16)
        ones_1 = sb("ones_1", [128, 128], BF16)
        uH = sb("uH", [128, 2, KC, 2], F32)
        hpH = sb("hpH", [128, 2, KC, 15], F32)
        ucS = sb("ucS", [128, KC, 32], F32)
        cvSo = sb("cvSo", [128, KC, 32], F32)
        hpSo = sb("hpSo", [128, KC, 64], F32)
        tmp15 = sb("tmp15", [128, 2, 16], F32)
        qsbS = sb("qsbS", [128, KC, 64], BF16)
        sgaS = sb("sgaS", [128, KC, 64], F32)
        doS = sb("doS", [128, 4, 3, 64], F32)
        rdS = sb("rdS", [128, 64], F32)
        pqS = sb("pqS", [128, 2, 32], BF16)
        banks = [es.enter_context(nc.psum_tensor("bank%d" % i, [128, 512], F32)) for i in range(8)]

        mg = mgR[:, :].rearrange("p (c t) -> p c t", c=KC)
        Kq = [mgR[:, i * 2048:(i + 1) * 2048].rearrange("p (b m d) -> p b m d", b=4, m=2) for i in range(2)]
        Vq = [mgR[:, 4096 + i * 2048:4096 + (i + 1) * 2048].rearrange("p (b m d) -> p b m d", b=4, m=2) for i in range(2)]
        KTq = scr[:, 13 * SLOTW:13 * SLOTW + 1024].bitcast(BF16).rearrange("p (b c m) -> p b c m", b=4, c=2)

        free_banks = list(range(8))

        def balloc():
            assert free_banks, "PSUM bank allocator exhausted (record order needs > 8 live banks)"
            return free_banks.pop(0)

        def bfree(*bs):
            for b in bs:
                assert b not in free_banks
                free_banks.append(b)

        def rB(b, lo=0, hi=512):
            return ("B%d" % b, lo * 4, hi * 4)

        def rS(i, lo=0, hi=SLOTW):
            return ("S%d" % i, lo * 4, hi * 4)

        def rSb(i, lo, hi):
            return ("S%d" % i, lo * 2, hi * 2)

        def rW(r):
            return ("W", r * 4096, (r + 1) * 4096)

        def rX(c, col, n):
            return ("xT", (c * NT + col) * 4, (c * NT + col + n) * 4)

        def rA(name, c, col, n):
            return (name, (c * NT + col) * 2, (c * NT + col + n) * 2)

        def rKT(l, c):
            return ("KTp", ((l * 8 + c) * 256) * 2, ((l * 8 + c + 1) * 256) * 2)

        def rV(l, mc, col0, n):
            return ("Vp", ((l * 2 + mc) * 1024 + col0) * 2, ((l * 2 + mc) * 1024 + col0 + n) * 2)

        def rG(name, idx, w):
            return (name, idx * w * 4, (idx + 1) * w * 4)

        R_KQ = [("mgR", i * 4096, (i + 1) * 4096) for i in range(2)]
        R_VQ = [("mgR", 8192 + i * 4096, 8192 + (i + 1) * 4096) for i in range(2)]

        def rKTq(hb=None):
            if hb is None:
                return [rS(13), ("S14", 0, 4096 - SLOTW * 4)]
            if hb == 0:
                return [("S13", 0, 2048)]
            return [("S13", 2048, SLOTW * 4), ("S14", 0, 4096 - SLOTW * 4)]

        def sview(i, n, dtype=F32, off=0):
            if dtype == F32:
                return slots[i][:, off:off + n]
            return slots[i][:, :].bitcast(BF16)[:, off:off + n]

        def vcol(base, idx):
            return vt[:, base + idx: base + idx + 1]

        def mm(out_ap, pairs, reads, writes, name="mm"):
            pairs = list(pairs)

            def fn(pe):
                n = len(pairs)
                ins = None
                for i, (l, r) in enumerate(pairs):
                    ins = pe.matmul(out_ap, l, r, start=(i == 0), stop=(i == n - 1))
                return ins
            return T.op("pe", fn, reads, writes, name=name)

        def mm_multi(groups, reads, writes, name="mmm"):
            groups = [(o, list(p)) for o, p in groups]

            def fn(pe):
                ins = None
                for out_ap, pairs in groups:
                    n = len(pairs)
                    for i, (l, r) in enumerate(pairs):
                        ins = pe.matmul(out_ap, l, r, start=(i == 0), stop=(i == n - 1))
                return ins
            return T.op("pe", fn, reads, writes, name=name)

        def tr_multi(items, reads, writes, name="tr"):
            items = list(items)

            def fn(pe):
                ins = None
                for o, i, idn in items:
                    ins = pe.transpose(o, i, idn)
                return ins
            return T.op("pe", fn, reads, writes, name=name)

        def act(out, in_, func, reads, writes, scale=None, bias=None, name="act"):
            def fn(a):
                kw = {}
                if scale is not None:
                    kw["scale"] = scale
                if bias is not None:
                    kw["bias"] = bias
                return a.activation(out=out, in_=in_, func=func, **kw)
            return T.op("act", fn, reads, writes, name=name)

        def rsqrt_eps(b, n, slot, name):
            act(sview(slot, n), banks[b][:, 0:n], AF.Sqrt, [rB(b, 0, n)], [rS(slot, 0, n)], bias=EPS, name=name + "_sqrt")
            ew("dve", "recip", [rS(slot, 0, n)], [rS(slot, 0, n)], out=sview(slot, n), in_=sview(slot, n), name=name)

        def ew(eng, kind, reads, writes, name="ew", **kw):
            def fn(e):
                if kind == "tt":
                    return e.tensor_tensor(out=kw["out"], in0=kw["in0"], in1=kw["in1"], op=kw["op"])
                if kind == "ts":
                    if "op1" in kw:
                        return e.tensor_scalar(out=kw["out"], in0=kw["in0"], scalar1=kw["s1"], scalar2=kw["s2"],
                                               op0=kw["op0"], op1=kw["op1"])
                    return e.tensor_scalar(out=kw["out"], in0=kw["in0"], scalar1=kw["s1"], scalar2=None, op0=kw["op0"])
                if kind == "stt":
                    return e.scalar_tensor_tensor(out=kw["out"], in0=kw["in0"], scalar=kw["scalar"], in1=kw["in1"],
                                                  op0=kw["op0"], op1=kw["op1"])
                if kind == "copy":
                    return e.tensor_copy(kw["out"], kw["in_"])
                if kind == "recip":
                    return e.reciprocal(kw["out"], kw["in_"])
                if kind == "memset":
                    return e.memset(kw["out"], kw["value"])
                raise ValueError(kind)
            return T.op(eng, fn, reads, writes, name=name)

        def dma(queue, out, in_, key, reads, writes, name="dma"):
            def fn(q):
                return q.dma_start(out=out, in_=in_)
            return T.op(queue, fn, reads, writes, dma_key=key, name=name)

        def evac(which, out, in_, reads, writes, name="ev"):
            if which == 0:
                return act(out, in_, AF.Copy, reads, writes, name=name)
            return ew("dve", "copy", reads, writes, out=out, in_=in_, name=name)

        def wsrc(ap2d, nkc):
            return ap2d.rearrange("(kc p) n -> p kc n", p=128), nkc

        def block_schedule(l, half):
            out = []
            if half == 0:
                for J in range(4):
                    out.append(("kvK%d" % J, wsrc(w_kv[l, :, J * 256:(J + 1) * 256], 8)))
                for J in range(4):
                    out.append(("kvV%d" % J, wsrc(w_kv[l, :, 1024 + J * 256:1024 + (J + 1) * 256], 8)))
                for h in range(4):
                    out.append(("sq%d" % h, wsrc(w_in[l, :, OFF_Q + h * 256: OFF_Q + (h + 1) * 256], 8)))
                    out.append(("sga%d" % h, wsrc(w_in[l, :, OFF_GA + h * 256: OFF_GA + (h + 1) * 256], 8)))
            for J in range(4):
                for nm, off in (("hc", OFF_HC), ("cc", OFF_CC), ("bc", OFF_BC), ("gc", OFF_GC)):
                    out.append(("%s%d" % (nm, J), wsrc(w_in[l, :, off + J * 256: off + (J + 1) * 256], 8)))
            for g in range(4):
                out.append(("hp%d" % g, wsrc(w_in[l, :, OFF_HP + g * 256: OFF_HP + (g + 1) * 256], 8)))
                out.append(("gp%d" % g, wsrc(w_in[l, :, OFF_GP + g * 256: OFF_GP + (g + 1) * 256], 8)))
                out.append(("pw%d" % g, wsrc(pool_w[l, g], 2)))
            for h in range(4):
                out.append(("q%d" % h, wsrc(w_in[l, :, OFF_Q + h * 256: OFF_Q + (h + 1) * 256], 8)))
                out.append(("ga%d" % h, wsrc(w_in[l, :, OFF_GA + h * 256: OFF_GA + (h + 1) * 256], 8)))
            for J in range(4):
                for nm, wb, off in (("c", w_bc, OFF_MC), ("p", w_bp, OFF_MP), ("a", w_ba, OFF_MA)):
                    out.append(("wb%s%d" % (nm, J), wsrc(wb[l, :, J * 256:(J + 1) * 256], 8)))
                    out.append(("m%s%d" % (nm, J), wsrc(w_in[l, :, off + J * 256: off + (J + 1) * 256], 8)))
            for J in range(4):
                out.append(("wo%d" % J, wsrc(w_o[l, :, J * 256:(J + 1) * 256], 8)))
            return [("L%dH%d_%s" % (l, half, t), s) for t, s in out]

        src_map = {}
        for l in range(2):
            for tag, s in block_schedule(l, 0):
                src_map[(l, tag.split("_", 1)[1])] = s
        cur = {"l": 0}
        dry_order = []
        sched = [] if dry else [("L%d_%s" % (l_, sfx), src_map[(l_, sfx)]) for (l_, sfx) in order]
        ws = {"issued": 0, "taken": 0, "released": 0}

        ws_rel = [False] * len(sched)
        ws_taken_idx = []

        def ws_pump():
            if dry:
                return
            while ws["issued"] < len(sched):
                i = ws["issued"]
                if i >= RING and not ws_rel[i - RING]:
                    break
                tag, (src, nkc) = sched[i]
                r = i % RING
                dma("pool", wring[:, r, 0:nkc, :], src, "W%d" % r, [], [rW(r)], name="w_" + tag)
                ws["issued"] += 1

        def ws_take(tag_suffix, lay=None, want_idx=False):
            i = ws["taken"]
            lay = cur["l"] if lay is None else lay
            ws["taken"] += 1
            if dry:
                dry_order.append((lay, tag_suffix))
                return (i % RING, i) if want_idx else i % RING
            tag = sched[i][0]
            assert tag == "L%d_%s" % (lay, tag_suffix), (tag, lay, tag_suffix)
            assert i < ws["issued"], "weight block not prefetched (ring too small for phase)"
            ws_taken_idx.append(i)
            return (i % RING, i) if want_idx else i % RING

        def ws_release(n=1):
            if dry:
                return
            for _ in range(n):
                i = ws_taken_idx.pop(0)
                ws_rel[i] = True
            ws_pump()

        def ws_release_idx(i):
            if dry:
                return
            ws_taken_idx.remove(i)
            ws_rel[i] = True
            ws_pump()

        def setup():
            dma("sp", vt[:, :], vecs[:, :], "vt", [], ["vt"], name="ld_vecs")
            dma("sp", ident[:, :], ident_in[:, :], "ident", [], ["ident"], name="ld_ident")
            ew("dve", "copy", ["ident"], ["identb"], out=identb[:, :], in_=ident[:, :])
            ew("dve", "memset", [], ["ones_m"], out=ones_m[:, :], value=1.0 / 1024.0)
            ew("dve", "memset", [], ["ones_1"], out=ones_1[:, :], value=1.0)
            ws_pump()

        def load_rows(src_rows, ntok, sl0):
            outs = []
            for hb in range(2):
                sl = sl0 + hb
                dma("sp", slots[sl][0:ntok, 0:512], src_rows[:, hb * 512:(hb + 1) * 512], "S%d" % sl, [], [rS(sl, 0, 512)], name="ld_rows")
                b = balloc()
                items = []
                for cc in range(4):
                    items.append((banks[b][:, cc * ntok:(cc + 1) * ntok],
                                  slots[sl][0:ntok, cc * 128:(cc + 1) * 128], ident[0:ntok, 0:ntok]))
                tr_multi(items, [rS(sl, 0, 512), "ident"], [rB(b, 0, 4 * ntok)], name="tr_in")
                outs.append(b)
            return outs

        def load_x(half):
            par = 0
            for ti, t in enumerate(HALVES[half]):
                nsub = t.n // 128 if t.kind == "P" else 1
                ntok = 128 if t.kind == "P" else 64
                for s in range(nsub):
                    src = x_p[t.row0 + s * 128: t.row0 + (s + 1) * 128, :] if t.kind == "P" else x_s[:, :]
                    bs = load_rows(src, ntok, 10 + 2 * par)
                    par = (par + 1) % 2
                    for hb, b in enumerate(bs):
                        c0 = t.col + s * ntok
                        outv = xT[:, 4 * hb:4 * hb + 4, c0:c0 + ntok]
                        inv = banks[b][:, 0:4 * ntok].rearrange("p (c t) -> p c t", c=4)
                        wr = [rX(c, c0, ntok) for c in range(4 * hb, 4 * hb + 4)]
                        evac(hb, outv, inv, [rB(b, 0, 4 * ntok)], wr, name="ev_x")
                        bfree(b)
                norm_tile(0, half, ti, t)

        def rms_stats(src_chunk, n, sq_slots, rstd_slot, src_reads, presq=False):
            for c in range(KC):
                if presq:
                    break
                si = sq_slots[c // 2]
                off = (c % 2) * 512
                act(sview(si, n, BF16, off=off), src_chunk(c), AF.Square, [src_reads(c)], [rSb(si, off, off + n)], name="sq")
            b = balloc()
            pairs = [(ones_m[:, :], sview(sq_slots[c // 2], n, BF16, off=(c % 2) * 512)) for c in range(KC)]
            rd = [rSb(sq_slots[c // 2], (c % 2) * 512, (c % 2) * 512 + n) for c in range(KC)]
            mm(banks[b][:, 0:n], pairs, rd + ["ones_m"], [rB(b, 0, n)], name="ss")
            rsqrt_eps(b, n, rstd_slot, "rstd")
            bfree(b)

        def presq_ok(l_prev, half):
            return l_prev >= 0 and not (half == 0 and l_prev == 0)

        def norm_phase(l, half):
            if l == 0:
                return
            for ti, t in enumerate(HALVES[half]):
                norm_tile(l, half, ti, t)

        def norm_tile(l, half, ti, t):
            if True:
                p = ti % 2
                sq = [4 * p + k for k in range(4)]
                rs = 8 + p
                rms_stats(lambda c: xT[:, c, t.col:t.col + t.n], t.n, sq, rs, lambda c: rX(c, t.col, t.n),
                          presq=(t.kind == "P" and presq_ok(l - 1, half)))
                for c in range(KC):
                    ew("dve", "stt", [rX(c, t.col, t.n), rS(rs, 0, t.n), "vt"], [rA("hT", c, t.col, t.n)],
                       out=hT[:, c, t.col:t.col + t.n], in0=xT[:, c, t.col:t.col + t.n],
                       scalar=vcol(V_NG, l * 8 + c), in1=sview(rs, t.n), op0=ALU.mult, op1=ALU.mult, name="h")

        def kv_steps(l):
            steps = []

            def memT(c):
                return slots[10 + c // 2][:, (c % 2) * 256:(c % 2) * 256 + 256]

            def r_memT(c):
                return rS(10 + c // 2, (c % 2) * 256, (c % 2) * 256 + 256)

            def memn(c):
                return sview(4 + c // 4, 256, BF16, off=(c % 4) * 256)

            def r_memn(c):
                return rSb(4 + c // 4, (c % 4) * 256, (c % 4) * 256 + 256)
            mn_reads = [r_memn(c) for c in range(KC)]

            def stg(mc, col0):
                sl = 10 + 2 * mc + col0 // 512
                o = col0 % 512
                return slots[sl][:, o:o + 256], rS(sl, o, o + 256)

            def st_load(s):
                bs = load_rows(mem[s * 128:(s + 1) * 128, :], 128, 6)
                for hb, b in enumerate(bs):
                    for c4 in range(2):
                        sl = 10 + 2 * hb + c4
                        outv = slots[sl][:, 0:512].rearrange("p (c t) -> p c t", c=2)[:, :, s * 128:(s + 1) * 128]
                        inv = banks[b][:, c4 * 256:(c4 + 1) * 256].rearrange("p (c t) -> p c t", c=2)
                        wr = [rS(sl, cl * 256 + s * 128, cl * 256 + (s + 1) * 128) for cl in range(2)]
                        evac(hb, outv, inv, [rB(b, c4 * 256, (c4 + 1) * 256)], wr, name="ev_mem")
                    bfree(b)

            def st_stats():
                for c in range(KC):
                    si, off = c // 2, (c % 2) * 512
                    act(sview(si, 256, BF16, off=off), memT(c), AF.Square, [r_memT(c)], [rSb(si, off, off + 256)], name="sqm")
                b = balloc()
                pairs = [(ones_m[:, :], sview(c // 2, 256, BF16, off=(c % 2) * 512)) for c in range(KC)]
                mm(banks[b][:, 0:256], pairs, [rSb(c // 2, (c % 2) * 512, (c % 2) * 512 + 256) for c in range(KC)] + ["ones_m"],
                   [rB(b, 0, 256)], name="ssm")
                rsqrt_eps(b, 256, 8, "rstdm")
                bfree(b)

            def st_memn():
                for c in range(KC):
                    ew("dve", "stt", [r_memT(c), rS(8, 0, 256), "vt"], [r_memn(c)],
                       out=memn(c), in0=memT(c), scalar=vcol(V_MG, l * 8 + c), in1=sview(8, 256),
                       op0=ALU.mult, op1=ALU.mult, name="memn")

            def st_K(J):
                r, ri = ws_take("kvK%d" % J, lay=l, want_idx=True)
                for jj in range(2):
                    b = balloc()
                    mm(banks[b][:, 0:256],
                       [(wring[:, r, kc, jj * 128:(jj + 1) * 128], memn(kc)) for kc in range(KC)],
                       mn_reads + [rW(r)], [rB(b, 0, 256)], name="kT")
                    act(KTp[:, l, 2 * J + jj, :], banks[b][:, 0:256], AF.Copy, [rB(b, 0, 256)], [rKT(l, 2 * J + jj)], name="ev_kT")
                    bfree(b)
                for mc in range(2):
                    b = balloc()
                    mm(banks[b][:, 0:256],
                       [(memn(kc)[:, mc * 128:(mc + 1) * 128], wring[:, r, kc, :]) for kc in range(KC)],
                       mn_reads + [rW(r)], [rB(b, 0, 256)], name="ktok")
                    o, ro = stg(mc, J * 256)
                    ew("dve", "copy", [rB(b, 0, 256)], [ro], out=o, in_=banks[b][:, 0:256], name="ev_ktok")
                    bfree(b)
                ws_release_idx(ri)

            def st_store(dst, nm):
                for mc in range(2):
                    for hf in range(2):
                        sl = 10 + 2 * mc + hf
                        dma("sp", dst[l, mc * 128:(mc + 1) * 128, hf * 512:(hf + 1) * 512], slots[sl][:, 0:512], "S%d" % sl,
                            [rS(sl, 0, 512)], [], name=nm)

            def st_V(J):
                r, ri = ws_take("kvV%d" % J, lay=l, want_idx=True)
                for mc in range(2):
                    b = balloc()
                    mm(banks[b][:, 0:256],
                       [(memn(kc)[:, mc * 128:(mc + 1) * 128], wring[:, r, kc, :]) for kc in range(KC)],
                       mn_reads + [rW(r)], [rB(b, 0, 256)], name="vtok")
                    act(Vp[:, l, mc, J * 256:(J + 1) * 256], banks[b][:, 0:256], AF.Copy, [rB(b, 0, 256)], [rV(l, mc, J * 256, 256)], name="ev_v")
                    o, ro = stg(mc, J * 256)
                    ew("dve", "copy", [rB(b, 0, 256)], [ro], out=o, in_=banks[b][:, 0:256], name="ev_vtok")
                    bfree(b)
                ws_release_idx(ri)

            steps.append(lambda: st_load(0))
            steps.append(lambda: st_load(1))
            steps.append(st_stats)
            steps.append(st_memn)
            for J in range(4):
                steps.append(lambda J=J: st_K(J))
            steps.append(lambda: st_store(o_mk, "st_mk"))
            for J in range(4):
                steps.append(lambda J=J: st_V(J))
            steps.append(lambda: st_store(o_mv, "st_mv"))
            return steps

        def kv_phase(l):
            for s in kv_steps(l):
                s()

        def store_rows(src_chunk, ncol, dst_rows, src_reads, sl0):
            for hb in range(2):
                b = balloc()
                items = [(banks[b][0:ncol, cc * 128:(cc + 1) * 128], src_chunk(4 * hb + cc), ident[:, :]) for cc in range(4)]
                rds = ["ident"] + [src_reads(4 * hb + cc) for cc in range(4)]
                tr_multi(items, rds, [rB(b)], name="tr_out")
                sl = sl0 + hb
                evac(hb, slots[sl][0:ncol, 0:512], banks[b][0:ncol, 0:512], [rB(b)], [rS(sl, 0, 512)], name="ev_out")
                bfree(b)
                dma("sp", dst_rows[:, hb * 512:(hb + 1) * 512], slots[sl][0:ncol, 0:512], "S%d" % sl, [rS(sl, 0, 512)], [], name="st_rows")

        bg = []
        bg_heads = set()

        def bg_drain(k):
            cnt = 0
            while bg and (k is None or cnt < k) and bg[0][0] in bg_heads:
                bg.pop(0)[1]()
                cnt += 1
            if k is None:
                assert not bg

        def sample_pre_att(l):
            t = HALVES[0][2]
            n = t.n
            hrd = [rA("hT", kc, t.col, n) for kc in range(KC)]
            rhs = [hT[:, kc, t.col:t.col + n] for kc in range(KC)]

            def st_LK(u):
                hh, qd, kb = u // 4, u % 4, u % 2
                ksrc = ck[l, 4 * qd:4 * qd + 4, :, hh, :].rearrange("b (m p) d -> p b m d", p=128)
                dma("pool", Kq[kb], ksrc, "Kq%d" % kb, [], [R_KQ[kb]], name="ld_K")

            def st_LV(u):
                hh, qd, kb = u // 4, u % 4, u % 2
                vsrc = cv[l, 4 * qd:4 * qd + 4, :, hh, :].rearrange("b (m p) d -> p b m d", p=128)
                dma("pool", Vq[kb], vsrc, "Vq%d" % kb, [], [R_VQ[kb]], name="ld_V")
            st_LK(0)
            st_LK(1)
            bg_heads.clear()
            bg_heads.add(-1)

            def head(hh, taken):
                rq, rga = taken
                for dc in range(2):
                    c = 2 * hh + dc
                    cs = slice(dc * 128, (dc + 1) * 128)
                    b = balloc()
                    mm(banks[b][:, 0:n], [(wring[:, rq, kc, cs], rhs[kc]) for kc in range(KC)],
                       hrd + [rW(rq)], [rB(b, 0, n)], name="mm_sq")
                    act(qsbS[:, c, :], banks[b][:, 0:n], AF.Copy, [rB(b, 0, n)], [rG("qsbS", c, 32)], scale=1.0 / 16.0, name="ev_sq")
                    bfree(b)
                    b = balloc()
                    mm(banks[b][:, 0:n], [(wring[:, rga, kc, cs], rhs[kc]) for kc in range(KC)],
                       hrd + [rW(rga)], [rB(b, 0, n)], name="mm_sga")
                    act(sgaS[:, c, :], banks[b][:, 0:n], AF.Tanh, [rB(b, 0, n)], [rG("sgaS", c, 64)], scale=0.5, name="tanh_sga")
                    ew("dve", "stt", [rG("sgaS", c, 64), rB(b, 0, n)], [rG("sgaS", c, 64)], out=sgaS[:, c, :],
                       in0=sgaS[:, c, :], scalar=1.0, in1=banks[b][:, 0:n], op0=ALU.add, op1=ALU.mult, name="silu2_s")
                    bfree(b)
                ws_release(2)
                bg_heads.add(hh)

            def st_T(u):
                kb = u % 2
                for hb in range(2):
                    bb = balloc()
                    items_ = []
                    bv = banks[bb][:, :].bitcast(BF16)
                    for bl in range(2):
                        for dc in range(2):
                            for mc in range(2):
                                c0 = (bl * 2 + dc) * 256 + mc * 128
                                items_.append((bv[:, c0:c0 + 128], Kq[kb][:, 2 * hb + bl, mc, dc * 128:(dc + 1) * 128], identb[:, :]))
                    tr_multi(items_, [R_KQ[kb], "identb"], [rB(bb)], name="tr_K")
                    outv = KTq[:, 2 * hb:2 * hb + 2, :, :]
                    inv = bv[:, 0:1024].rearrange("p (b c m) -> p b c m", b=2, c=2)
                    evac(0, outv, inv, [rB(bb)], rKTq(hb), name="ev_KT")
                    bfree(bb)

            def st_S(u):
                hh, qd, kb = u // 4, u % 4, u % 2
                b_s = balloc()
                groups = []
                for bl in range(4):
                    bgi = 4 * qd + bl
                    for mc in range(2):
                        c0 = bl * 8 + mc * 4
                        groups.append((banks[b_s][:, c0:c0 + 4],
                                       [(KTq[:, bl, dc, mc * 128:(mc + 1) * 128], qsbS[:, 2 * hh + dc, bgi:64:16]) for dc in range(2)]))
                mm_multi(groups, [rG("qsbS", 2 * hh, 32), rG("qsbS", 2 * hh + 1, 32)] + rKTq(), [rB(b_s, 0, 32)], name="mm_ss")
                act(pqS[:, kb, :], banks[b_s][:, 0:32], AF.Exp, [rB(b_s, 0, 32)], [rG("pqS", kb, 16)], name="exp_s")
                bfree(b_s)

            def st_O(u):
                hh, qd, kb = u // 4, u % 4, u % 2
                pq = pqS[:, kb, :]
                r_pq = rG("pqS", kb, 16)
                b = balloc()
                pq4 = pq.rearrange("p (b m t) -> p b m t", b=4, m=2)
                mm(banks[b][:, 0:16], [(ones_1[:, :], pq4[:, :, mc, :]) for mc in range(2)], [r_pq, "ones_1"], [rB(b, 0, 16)], name="mm_dens")
                for dc in range(2):
                    groups = []
                    for bl in range(4):
                        groups.append((banks[b][:, 16 + dc * 16 + bl * 4: 16 + dc * 16 + bl * 4 + 4],
                                       [(Vq[kb][:, bl, mc, dc * 128:(dc + 1) * 128], pq[:, bl * 8 + mc * 4: bl * 8 + mc * 4 + 4])
                                        for mc in range(2)]))
                    mm_multi(groups, [r_pq, R_VQ[kb]], [rB(b, 16 + dc * 16, 32 + dc * 16)], name="mm_os")
                outv = doS[:, hh, :, :].rearrange("p w (t b) -> p w b t", b=16)[:, :, 4 * qd:4 * qd + 4, :]
                inv = banks[b][:, 0:48].rearrange("p (w b t) -> p w b t", w=3, b=4)
                act(outv, inv, AF.Copy, [rB(b, 0, 48)], [rG("doS", hh, 192)], name="ev_do")
                bfree(b)

            def st_E(hh):
                ew("dve", "recip", [rG("doS", hh, 192)], ["rdS"], out=rdS[:, :], in_=doS[:, hh, 0, :], name="rden_s")
                for dc in range(2):
                    c = 2 * hh + dc
                    ew("dve", "stt", [rG("sgaS", c, 64), "rdS"], [rG("sgaS", c, 64)], out=sgaS[:, c, :],
                       in0=sgaS[:, c, :], scalar=0.5, in1=rdS[:, :], op0=ALU.mult, op1=ALU.mult, name="gg_s")
                    ew("dve", "tt", [rG("doS", hh, 192), rG("sgaS", c, 64)], [rA("aA", c, t.col, n)],
                       out=aA[:, c, t.col:t.col + n], in0=doS[:, hh, 1 + dc, :], in1=sgaS[:, c, :], op=ALU.mult, name="aA_s")

            def macro(k):
                if 0 <= k < 16:
                    st_S(k)
                if 0 <= k - 1 < 16:
                    st_O(k - 1)
                    if (k - 1) % 4 == 3:
                        st_E((k - 1) // 4)
                if 0 <= k + 1 < 16:
                    st_LV(k + 1)
                if k + 1 < 16:
                    st_T(k + 1)
                if k + 3 < 16:
                    st_LK(k + 3)
            for k in range(-1, 17):
                bg.append((min(max(k, -1), 15) // 4 if k >= 0 else -1, lambda k=k: macro(k)))
            return head

        def conv_phase(l, half):
            tiles = HALVES[half]
            if half == 0:
                bs = load_rows(st_c[l], 32, 10)
                for hb, b in enumerate(bs):
                    outv = ucS[:, 4 * hb:4 * hb + 4, :]
                    inv = banks[b][:, 0:128].rearrange("p (c t) -> p c t", c=4)
                    ew("dve", "copy", [rB(b, 0, 128)], [("ucS", hb * 512, (hb + 1) * 512)], out=outv, in_=inv, name="ev_ucS")
                    bfree(b)
            it = 0
            pcount = 0
            deferred = []
            spre_head = sample_pre_att(l) if half == 0 else None
            for J in range(4):
                rh, rc_, rb, rg = ws_take("hc%d" % J), ws_take("cc%d" % J), ws_take("bc%d" % J), ws_take("gc%d" % J)
                for jj in range(2):
                    j = 2 * J + jj
                    cs = slice(jj * 128, (jj + 1) * 128)
                    if J == 0 and jj == 0:
                        ctiles = [t for t in tiles if t.kind == "P"][:1] + [t for t in tiles if t.kind == "S"] + [t for t in tiles if t.kind == "P"][1:]
                    else:
                        ctiles = [t for t in tiles if t.kind == "S"] + [t for t in tiles if t.kind == "P"]
                    for ti, t in enumerate(ctiles):
                        last_item = (jj == 1 and ti == len(ctiles) - 1)
                        n, su = t.n, t.su
                        p = it % 2
                        it += 1
                        S_hc, S_cc, S_u, S_sg, S_bc = [5 * p + k for k in range(5)]
                        hrd = [rA("hT", kc, t.col, n) for kc in range(KC)]
                        rhs = [hT[:, kc, t.col:t.col + n] for kc in range(KC)]
                        b_h, b_c, b_b, b_g = balloc(), balloc(), balloc(), balloc()
                        for bb, r in ((b_h, rh), (b_c, rc_), (b_b, rb), (b_g, rg)):
                            mm(banks[bb][:, 0:n], [(wring[:, r, kc, cs], rhs[kc]) for kc in range(KC)],
                               hrd + [rW(r)], [rB(bb, 0, n)], name="mm_conv")
                            if last_item:
                                ws_release(1)
                        ue = slots[S_u]
                        r_head = rS(S_u, 0, 2 * su)
                        r_body = rS(S_u, 2 * su, 2 * su + n)
                        r_all = rS(S_u, 0, 2 * su + n)
                        if t.kind == "S":
                            ew("pool", "copy", [rG("ucS", j, 32)], [r_head], out=ue[:, 0:32], in_=ucS[:, j, :], name="halo_in")
                        elif t.pos0 == 0:
                            ew("pool", "memset", [], [r_head], out=ue[:, 0:2], value=0.0, name="halo_z")
                        else:
                            ew("pool", "copy", [rG("uH", l * 8 + j, 2)], [r_head], out=ue[:, 0:2], in_=uH[:, l, j, :], name="halo_in")
                        act(sview(S_hc, n), banks[b_h][:, 0:n], AF.Copy, [rB(b_h, 0, n)], [rS(S_hc, 0, n)], name="ev_hc")
                        act(sview(S_cc, n), banks[b_c][:, 0:n], AF.Copy, [rB(b_c, 0, n)], [rS(S_cc, 0, n)], name="ev_cc")
                        act(sview(S_bc, n), banks[b_b][:, 0:n], AF.Copy, [rB(b_b, 0, n)], [rS(S_bc, 0, n)], name="ev_bc")
                        act(sview(S_sg, n), banks[b_g][:, 0:n], AF.Silu, [rB(b_g, 0, n)], [rS(S_sg, 0, n)], name="silu_gc")
                        bfree(b_h, b_c, b_b, b_g)
                        ew("dve", "tt", [rS(S_cc, 0, n), rS(S_hc, 0, n)], [r_body], out=ue[:, 2 * su:2 * su + n],
                           in0=sview(S_cc, n), in1=sview(S_hc, n), op=ALU.mult, name="u")
                        if t.kind == "S":
                            ew("pool", "copy", [rS(S_u, n, n + 32)], [rG("cvSo", j, 32)], out=cvSo[:, j, :], in_=ue[:, n:n + 32], name="halo_out")
                        else:
                            ew("pool", "copy", [rS(S_u, n, n + 2)], [rG("uH", l * 8 + j, 2)], out=uH[:, l, j, :], in_=ue[:, n:n + 2], name="halo_out")
                        ya = sview(S_hc, n)
                        yb = sview(S_cc, n)
                        ew("dve", "ts", [r_all, "vt"], [rS(S_hc, 0, n)], out=ya, in0=ue[:, 0:n], s1=vcol(V_CW, l * 24 + 0 * 8 + j),
                           s2=None, op0=ALU.mult, name="y0")
                        ew("dve", "stt", [r_all, "vt", rS(S_hc, 0, n)], [rS(S_cc, 0, n)], out=yb, in0=ue[:, su:su + n],
                           scalar=vcol(V_CW, l * 24 + 1 * 8 + j), in1=ya, op0=ALU.mult, op1=ALU.add, name="y1")
                        ew("dve", "stt", [r_all, "vt", rS(S_cc, 0, n)], [rS(S_hc, 0, n)], out=ya, in0=ue[:, 2 * su:2 * su + n],
                           scalar=vcol(V_CW, l * 24 + 2 * 8 + j), in1=yb, op0=ALU.mult, op1=ALU.add, name="y2")
                        ew("dve", "tt", [rS(S_bc, 0, n), rS(S_sg, 0, n)], [rS(S_cc, 0, n)], out=yb, in0=sview(S_bc, n), in1=sview(S_sg, n),
                           op=ALU.mult, name="bsg")
                        ew("dve", "tt", [rS(S_hc, 0, n), rS(S_cc, 0, n)], [rA("aC", j, t.col, n)], out=aC[:, j, t.col:t.col + n],
                           in0=ya, in1=yb, op=ALU.mult, name="aC")
                        if t.kind == "P":
                            if deferred:
                                deferred.pop(0)()
                            pcount += 1
                            if pcount % 2 == 0:
                                bg_drain(1)
                if half == 0:
                    taken = (ws_take("sq%d" % J), ws_take("sga%d" % J))
                    deferred.append(lambda J=J, taken=taken: spre_head(J, taken))
            while deferred:
                deferred.pop(0)()
            if half == 0:
                store_rows(lambda c: cvSo[:, c, :], 32, o_cvs[l], lambda c: rG("cvSo", c, 32), 10)
            else:
                store_rows(lambda c: uH[:, l, c, :], 2, o_cvp[l], lambda c: rG("uH", l * 8 + c, 2), 10)

        def pool_phase(l, half):
            tiles = HALVES[half]
            if half == 0:
                dma("sp", o_pls[l, 0:176, :], st_p[l, 64:240, :], "d2d", [], [], name="d2d_pool")

            def state_dma(g):
                dma("sp", slots[12][0:128, 0:256], st_p[l, 0:128, g * 256:(g + 1) * 256], "S12", [], [rS(12, 0, 256)], name="ld_pst0")
                dma("sp", slots[12][0:112, 256:512], st_p[l, 128:240, g * 256:(g + 1) * 256], "S12", [], [rS(12, 256, 512)], name="ld_pst1")

            def state_to_halo(S_e):
                b = balloc()
                its = []
                for jj in range(2):
                    for rg, (r0, nr) in enumerate(((0, 128), (128, 112))):
                        its.append((banks[b][:, jj * 240 + r0: jj * 240 + r0 + nr],
                                    slots[12][0:nr, rg * 256 + jj * 128: rg * 256 + (jj + 1) * 128], ident[0:nr, 0:nr]))
                tr_multi(its, [rS(12, 0, 512), "ident"], [rB(b, 0, 480)], name="tr_pst")
                for jj in range(2):
                    act(slots[S_e[jj]][:, 0:240], banks[b][:, jj * 240:(jj + 1) * 240], AF.Copy, [rB(b, 0, 480)],
                        [rS(S_e[jj], 0, 240)], name="phalo_in_s")
                bfree(b)

            items = []
            for g in (3, 0, 2, 1):
                for ti, t in enumerate(tiles):
                    items.append(dict(g=g, t=t, first=(ti == 0), last=(ti == len(tiles) - 1), idx=len(items)))
            wslots = {}

            def stage_a(itm):
                g, t = itm["g"], itm["t"]
                if itm["first"]:
                    wslots[g] = (ws_take("hp%d" % g), ws_take("gp%d" % g), ws_take("pw%d" % g))
                    if half == 0:
                        state_dma(g)
                rhp, rgp, rpw = wslots[g]
                w, m = WIN[g], g + 1
                n, su = t.n, t.su
                L = 15 * su + n
                p = itm["idx"] % 2
                S_e = [0 + 2 * p, 1 + 2 * p]
                S_sg = [4 + 2 * p, 5 + 2 * p]
                S_mx = 8 + p
                S_tA, S_tB = 10, 11
                hrd = [rA("hT", kc, t.col, n) for kc in range(KC)]
                rhs = [hT[:, kc, t.col:t.col + n] for kc in range(KC)]
                b_hp = [balloc(), balloc()]
                b_gp = [balloc(), balloc()]
                for jj in range(2):
                    cs = slice(jj * 128, (jj + 1) * 128)
                    mm(banks[b_hp[jj]][:, 0:n], [(wring[:, rhp, kc, cs], rhs[kc]) for kc in range(KC)],
                       hrd + [rW(rhp)], [rB(b_hp[jj], 0, n)], name="mm_hp")
                    mm(banks[b_gp[jj]][:, 0:n], [(wring[:, rgp, kc, cs], rhs[kc]) for kc in range(KC)],
                       hrd + [rW(rgp)], [rB(b_gp[jj], 0, n)], name="mm_gp")
                mx = [sview(S_mx, n, BF16, off=jj * 512) for jj in range(2)]
                r_mx = [rSb(S_mx, jj * 512, jj * 512 + n) for jj in range(2)]
                itm.update(mx=mx, r_mx=r_mx, S_sg=S_sg, rpw=rpw)
                if t.kind == "S":
                    state_to_halo(S_e)
                for jj in range(2):
                    j = 2 * g + jj
                    e = slots[S_e[jj]]
                    r_head = rS(S_e[jj], 0, 15 * su)
                    r_body = rS(S_e[jj], 15 * su, L)
                    r_all = rS(S_e[jj], 0, L)
                    if t.kind == "S":
                        pass
                    elif t.pos0 == 0:
                        ew("pool", "memset", [], [r_head], out=e[:, 0:15], value=0.0, name="phalo_z")
                    else:
                        ew("pool", "copy", [rG("hpH", l * 8 + j, 15)], [r_head], out=e[:, 0:15], in_=hpH[:, l, j, :], name="phalo_in")
                    act(e[:, 15 * su:L], banks[b_hp[jj]][:, 0:n], AF.Copy, [rB(b_hp[jj], 0, n)], [r_body], name="ev_hp")
                    if t.kind == "S":
                        ew("pool", "copy", [rS(S_e[jj], 240, 304)], [rG("hpSo", j, 64)], out=hpSo[:, j, :], in_=e[:, 240:304], name="phalo_out")
                    else:
                        ew("pool", "copy", [rS(S_e[jj], n, n + 15)], [rG("hpH", l * 8 + j, 15)], out=hpH[:, l, j, :], in_=e[:, n:n + 15], name="phalo_out")
                    act(sview(S_sg[jj], n), banks[b_gp[jj]][:, 0:n], AF.Silu, [rB(b_gp[jj], 0, n)], [rS(S_sg[jj], 0, n)], name="silu_gp")
                bfree(*b_hp)
                bfree(*b_gp)
                for jj in range(2):
                    e = slots[S_e[jj]]
                    r_all = rS(S_e[jj], 0, L)
                    lo = [0] * (m + 1)
                    lo[m] = 15
                    for k in range(m, 0, -1):
                        lo[k - 1] = lo[k] - (1 << (k - 1))
                    cur, cur_rd = e, r_all
                    tsl = [S_tA, S_tB]
                    for k in range(1, m + 1):
                        sh = (1 << (k - 1)) * su
                        a0 = lo[k] * su
                        dst_sl = tsl[k % 2]
                        dst = slots[dst_sl]
                        ew("dve", "tt", [cur_rd], [rS(dst_sl, a0, L)], out=dst[:, a0:L], in0=cur[:, a0:L], in1=cur[:, a0 - sh:L - sh],
                           op=ALU.add, name="win")
                        cur, cur_rd = dst, rS(dst_sl, 0, L)
                    ew("dve", "stt", [cur_rd, r_all], [r_mx[jj]], out=mx[jj], in0=cur[:, 15 * su:L], scalar=1.0 / w,
                       in1=e[:, 15 * su:L], op0=ALU.mult, op1=ALU.subtract, name="mixed")
                    if t.kind == "P" and t.pos0 == 0:
                        ew("dve", "tt", [cur_rd, "vt"], [rG("tmp15", jj, 16)], out=tmp15[:, jj, 0:15], in0=cur[:, 15:30],
                           in1=vt[:, V_RC + g * 15: V_RC + (g + 1) * 15], op=ALU.mult, name="fix1")
                        ew("dve", "tt", [rG("tmp15", jj, 16), r_all], [rSb(S_mx, jj * 512, jj * 512 + 15)], out=mx[jj][:, 0:15],
                           in0=tmp15[:, jj, 0:15], in1=e[:, 15:30], op=ALU.subtract, name="fix2")

            def stage_b(itm):
                g, t = itm["g"], itm["t"]
                n = t.n
                mx, r_mx, S_sg, rpw = itm["mx"], itm["r_mx"], itm["S_sg"], itm["rpw"]
                for dd in range(2):
                    b = balloc()
                    mm(banks[b][:, 0:n], [(wring[:, rpw, jj, dd * 128:(dd + 1) * 128], mx[jj]) for jj in range(2)],
                       r_mx + [rW(rpw)], [rB(b, 0, n)], name="mm_pool")
                    j = 2 * g + dd
                    ew("dve", "stt", [rB(b, 0, n), "vt", rS(S_sg[dd], 0, n)], [rA("aP", j, t.col, n)], out=aP[:, j, t.col:t.col + n],
                       in0=banks[b][:, 0:n], scalar=vcol(V_PS, l * 8 + j), in1=sview(S_sg[dd], n),
                       op0=ALU.mult, op1=ALU.mult, name="aP")
                    bfree(b)
                if itm["last"]:
                    ws_release(3)
                if t.kind == "P":
                    bg_drain(1)

            for k in range(len(items) + 1):
                if k < len(items):
                    stage_a(items[k])
                if k >= 1:
                    stage_b(items[k - 1])
            if half == 0:
                store_rows(lambda c: hpSo[:, c, :], 64, o_pls[l, 176:240, :], lambda c: rG("hpSo", c, 64), 10)
            else:
                store_rows(lambda c: hpH[:, l, c, :], 15, o_plp[l], lambda c: rG("hpH", l * 8 + c, 15), 10)

        def att_phase(l, half):
            tiles = [t for t in HALVES[half] if t.kind == "P"]
            items = []
            for hh in range(4):
                for ti, t in enumerate(tiles):
                    items.append(dict(hh=hh, t=t, first=(ti == 0), last=(ti == len(tiles) - 1), idx=len(items)))
            wsl = {}

            def stage_a(itm):
                hh, t = itm["hh"], itm["t"]
                if itm["first"]:
                    wsl[hh] = (ws_take("q%d" % hh), ws_take("ga%d" % hh))
                rq, rga = wsl[hh]
                n = t.n
                i = itm["idx"]
                S_q = i % 2
                S_sga = [6 + 2 * (i % 3), 7 + 2 * (i % 3)]
                hrd = [rA("hT", kc, t.col, n) for kc in range(KC)]
                rhs = [hT[:, kc, t.col:t.col + n] for kc in range(KC)]
                qsb = [sview(S_q, n, BF16, off=dc * 512) for dc in range(2)]
                qrd = [rSb(S_q, dc * 512, dc * 512 + n) for dc in range(2)]
                itm.update(qsb=qsb, qrd=qrd, S_sga=S_sga, S_p=2 + i % 2, S_rd=4 + i % 2)
                for dc in range(2):
                    cs = slice(dc * 128, (dc + 1) * 128)
                    b = balloc()
                    mm(banks[b][:, 0:n], [(wring[:, rq, kc, cs], rhs[kc]) for kc in range(KC)],
                       hrd + [rW(rq)], [rB(b, 0, n)], name="mm_q")
                    act(qsb[dc], banks[b][:, 0:n], AF.Copy, [rB(b, 0, n)], [qrd[dc]], scale=1.0 / 16.0, name="ev_q")
                    bfree(b)
                for dc in range(2):
                    cs = slice(dc * 128, (dc + 1) * 128)
                    b = balloc()
                    mm(banks[b][:, 0:n], [(wring[:, rga, kc, cs], rhs[kc]) for kc in range(KC)],
                       hrd + [rW(rga)], [rB(b, 0, n)], name="mm_ga")
                    act(sview(S_sga[dc], n), banks[b][:, 0:n], AF.Tanh, [rB(b, 0, n)], [rS(S_sga[dc], 0, n)], scale=0.5, name="tanh_ga")
                    ew("dve", "stt", [rS(S_sga[dc], 0, n), rB(b, 0, n)], [rS(S_sga[dc], 0, n)], out=sview(S_sga[dc], n),
                       in0=sview(S_sga[dc], n), scalar=1.0, in1=banks[b][:, 0:n], op0=ALU.add, op1=ALU.mult, name="silu2")
                    bfree(b)
                if itm["last"]:
                    ws_release(2)
                bg_drain(1)

            def stage_b(itm):
                hh, t = itm["hh"], itm["t"]
                n = t.n
                qsb, qrd, S_p = itm["qsb"], itm["qrd"], itm["S_p"]
                if t.kind == "P":
                    pT = [sview(S_p, n, BF16, off=mc * 512) for mc in range(2)]
                    prd = [rSb(S_p, mc * 512, mc * 512 + n) for mc in range(2)]
                    itm.update(pT=pT, prd=prd)
                    for mc in range(2):
                        b = balloc()
                        mm(banks[b][:, 0:n],
                           [(KTp[:, l, 2 * hh + dc, mc * 128:(mc + 1) * 128], qsb[dc]) for dc in range(2)],
                           qrd + [rKT(l, 2 * hh), rKT(l, 2 * hh + 1)], [rB(b, 0, n)], name="mm_s")
                        act(pT[mc], banks[b][:, 0:n], AF.Exp, [rB(b, 0, n)], [prd[mc]], name="exp")
                        bfree(b)
                    return
                raise AssertionError("sample tile is handled by the background sample-attention steps")

            def stage_c(itm):
                hh, t = itm["hh"], itm["t"]
                n = t.n
                S_sga, S_rd = itm["S_sga"], itm["S_rd"]
                if t.kind == "P":
                    pT, prd = itm["pT"], itm["prd"]
                    b_den = balloc()
                    b_o = [balloc(), balloc()]
                    mm(banks[b_den][:, 0:n], [(ones_1[:, :], pT[mc]) for mc in range(2)], prd + ["ones_1"], [rB(b_den, 0, n)], name="mm_den")
                    for dc in range(2):
                        mm(banks[b_o[dc]][:, 0:n],
                           [(Vp[:, l, mc, hh * 256 + dc * 128: hh * 256 + (dc + 1) * 128], pT[mc]) for mc in range(2)],
                           prd + [rV(l, mc, hh * 256, 256) for mc in range(2)], [rB(b_o[dc], 0, n)], name="mm_o")
                ew("dve", "recip", [rB(b_den, 0, n)], [rS(S_rd, 0, n)], out=sview(S_rd, n), in_=banks[b_den][:, 0:n], name="rden")
                bfree(b_den)
                for dc in range(2):
                    ew("dve", "stt", [rS(S_sga[dc], 0, n), rS(S_rd, 0, n)], [rS(S_sga[dc], 0, n)], out=sview(S_sga[dc], n),
                       in0=sview(S_sga[dc], n), scalar=0.5, in1=sview(S_rd, n), op0=ALU.mult, op1=ALU.mult, name="gg")
                    ew("dve", "tt", [rB(b_o[dc], 0, n), rS(S_sga[dc], 0, n)], [rA("aA", 2 * hh + dc, t.col, n)],
                       out=aA[:, 2 * hh + dc, t.col:t.col + n], in0=banks[b_o[dc]][:, 0:n], in1=sview(S_sga[dc], n),
                       op=ALU.mult, name="aA")
                    bfree(b_o[dc])

            ni = len(items)
            for k in range(ni + 2):
                if k < ni:
                    stage_a(items[k])
                if 0 <= k - 1 < ni:
                    stage_b(items[k - 1])
                if 0 <= k - 2 < ni:
                    stage_c(items[k - 2])

        def merge_phase(l, half):
            bg_drain(None)
            tiles = HALVES[half]

            def acc(jj, t):
                sl = 6 * jj + t.col // 512
                return slots[sl][:, 0:t.n], rS(sl, 0, t.n)
            it = 0
            for J in range(4):
                for bi, (nm, abuf, aname) in enumerate((("c", aC, "aC"), ("p", aP, "aP"), ("a", aA, "aA"))):
                    rw, rm = ws_take("wb%s%d" % (nm, J)), ws_take("m%s%d" % (nm, J))
                    for jj in range(2):
                        j = 2 * J + jj
                        cs = slice(jj * 128, (jj + 1) * 128)
                        for t in tiles:
                            n = t.n
                            p = it % 2
                            it += 1
                            S_sg, S_tmp = 3 + p, 9 + p
                            b_br, b_m = balloc(), balloc()
                            mm(banks[b_br][:, 0:n], [(wring[:, rw, kc, cs], abuf[:, kc, t.col:t.col + n]) for kc in range(KC)],
                               [rA(aname, kc, t.col, n) for kc in range(KC)] + [rW(rw)], [rB(b_br, 0, n)], name="mm_br")
                            mm(banks[b_m][:, 0:n], [(wring[:, rm, kc, cs], hT[:, kc, t.col:t.col + n]) for kc in range(KC)],
                               [rA("hT", kc, t.col, n) for kc in range(KC)] + [rW(rm)], [rB(b_m, 0, n)], name="mm_m")
                            act(sview(S_sg, n), banks[b_m][:, 0:n], AF.Sigmoid, [rB(b_m, 0, n)], [rS(S_sg, 0, n)], name="sig")
                            aap, ares = acc(jj, t)
                            if bi == 0:
                                ew("dve", "tt", [rB(b_br, 0, n), rS(S_sg, 0, n)], [ares], out=aap, in0=banks[b_br][:, 0:n],
                                   in1=sview(S_sg, n), op=ALU.mult, name="mg0")
                            else:
                                ew("dve", "tt", [rB(b_br, 0, n), rS(S_sg, 0, n)], [rS(S_tmp, 0, n)], out=sview(S_tmp, n),
                                   in0=banks[b_br][:, 0:n], in1=sview(S_sg, n), op=ALU.mult, name="mgt")
                                if bi == 1:
                                    ew("dve", "tt", [rS(S_tmp, 0, n), ares], [ares], out=aap, in0=sview(S_tmp, n), in1=aap,
                                       op=ALU.add, name="mg1")
                                else:
                                    ew("dve", "tt", [rS(S_tmp, 0, n), ares], [rA("mgR", j, t.col, n)], out=mg[:, j, t.col:t.col + n],
                                       in0=sview(S_tmp, n), in1=aap, op=ALU.add, name="mg2")
                            bfree(b_br, b_m)
                    ws_release(2)

        def out_phase(l, half):
            tiles = HALVES[half]
            kvq = kv_steps(1) if (half == 0 and l == 0) else []
            for J in range(4):
                r, ri = ws_take("wo%d" % J, want_idx=True)
                for jj in range(2):
                    j = 2 * J + jj
                    cs = slice(jj * 128, (jj + 1) * 128)
                    for t in tiles:
                        n = t.n
                        b = balloc()
                        mm(banks[b][:, 0:n], [(wring[:, r, kc, cs], mg[:, kc, t.col:t.col + n]) for kc in range(KC)],
                           [rA("mgR", kc, t.col, n) for kc in range(KC)] + [rW(r)], [rB(b, 0, n)], name="mm_out")
                        xv = xT[:, j, t.col:t.col + n]
                        ew("dve", "tt", [rB(b, 0, n), rX(j, t.col, n)], [rX(j, t.col, n)], out=xv, in0=banks[b][:, 0:n], in1=xv,
                           op=ALU.add, name="resid")
                        bfree(b)
                        if t.kind == "P" and presq_ok(l, half):
                            ti = tiles.index(t)
                            si, off = 4 * (ti % 2) + j // 2, (j % 2) * 512
                            act(sview(si, n, BF16, off=off), xv, AF.Square, [rX(j, t.col, n)], [rSb(si, off, off + n)], name="sq_pre")
                        if t.kind == "P" and kvq:
                            kvq.pop(0)()
                ws_release_idx(ri)
            while kvq:
                kvq.pop(0)()

        def final_phase(half):
            tl = HALVES[half]
            for ti, t in enumerate(tl):
                p = ti % 2
                sq = [4 * p + k for k in range(4)]
                rms_stats(lambda c: xT[:, c, t.col:t.col + t.n], t.n, sq, 8 + ti, lambda c: rX(c, t.col, t.n),
                          presq=(t.kind == "P" and presq_ok(1, half)))
            par = 0
            for ti, t in enumerate(tl):
                rs = 8 + ti
                for c in range(KC):
                    xv = xT[:, c, t.col:t.col + t.n]
                    ew("dve", "stt", [rX(c, t.col, t.n), rS(rs, 0, t.n), "vt"], [rX(c, t.col, t.n)],
                       out=xv, in0=xv, scalar=vcol(V_FG, c), in1=sview(rs, t.n), op0=ALU.mult, op1=ALU.mult, name="yT")
                nsub = t.n // 128 if t.kind == "P" else 1
                ntok = 128 if t.kind == "P" else 64
                for s in range(nsub):
                    c0 = t.col + s * ntok
                    dst = y_p[t.row0 + s * 128: t.row0 + (s + 1) * 128, :] if t.kind == "P" else y_s[:, :]
                    store_rows(lambda c: xT[:, c, c0:c0 + ntok], ntok, dst, lambda c: rX(c, c0, ntok), 11 + 2 * par)
                    par ^= 1

        phases = [("setup", 0, setup)]
        for half in range(2):
            phases.append(("load_x%d" % half, 0, lambda half=half: load_x(half)))
            for l in range(2):
                if half == 0 and l == 0:
                    phases.append(("kv%d" % l, l, lambda l=l: kv_phase(l)))
                phases.append(("norm", l, lambda l=l, half=half: norm_phase(l, half)))
                phases.append(("conv", l, lambda l=l, half=half: conv_phase(l, half)))
                phases.append(("pool", l, lambda l=l, half=half: pool_phase(l, half)))
                phases.append(("att", l, lambda l=l, half=half: att_phase(l, half)))
                phases.append(("merge", l, lambda l=l, half=half: merge_phase(l, half)))
                phases.append(("out", l, lambda l=l, half=half: out_phase(l, half)))
            phases.append(("final%d" % half, 0, lambda half=half: final_phase(half)))
        for pi, (pname, pl, pf) in enumerate(phases):
            if STOP_AFTER is not None and pi >= STOP_AFTER:
                break
            cur["l"] = pl
            pf()
        if dry:
            return dry_order
        if STOP_AFTER is None:
            assert ws["taken"] == len(sched), (ws["taken"], len(sched))

        all_keys = sorted(T.dma_cnt.keys())
        T.finalize()

        sems = {e: es.enter_context(nc.semaphore("sem_" + e)) for e in ("pe", "act", "dve", "pool")}
        dma_sems = {k: es.enter_context(nc.semaphore("dsem_" + str(k))) for k in all_keys}
        block = es.enter_context(nc.Block())

        @block.tensor
        def _(e):
            T.emit("pe", e, sems, dma_sems)

        @block.scalar
        def _(e):
            T.emit("act", e, sems, dma_sems)

        @block.vector
        def _(e):
            T.emit("dve", e, sems, dma_sems)

        @block.gpsimd
        def _(e):
            T.emit("pool", e, sems, dma_sems)

        @block.sync
        def _(e):
            T.emit("sp", e, sems, dma_sems)
            for k in all_keys:
                e.wait_ge(dma_sems[k], 16 * T.dma_cnt[k])
    return nc


_CACHE = {}


def _pack_vecs(norm_g, conv_w, pool_scale, mem_norm_g, final_norm_g):
    v = np.zeros((128, NV), np.float32)

    def lay(a):
        a = np.asarray(a, np.float32)
        lead = a.shape[:-1]
        return np.moveaxis(a.reshape(lead + (8, 128)), -1, 0).reshape(128, -1)
    v[:, V_NG:V_NG + 16] = lay(norm_g)
    v[:, V_CW:V_CW + 48] = lay(conv_w)
    v[:, V_PS:V_PS + 16] = lay(pool_scale)
    v[:, V_MG:V_MG + 16] = lay(mem_norm_g)
    v[:, V_FG:V_FG + 8] = lay(final_norm_g)
    rc = np.zeros((4, 15), np.float32)
    for g, w in enumerate(WIN):
        rc[g] = 1.0 / np.minimum(np.arange(15) + 1, w)
    v[:, V_RC:V_RC + 60] = rc.reshape(1, 60)
    return v


def kernel(x_prompt, x_sample, mem_prompt, cache_mem_k, cache_mem_v, state_conv, state_pool,
           norm_g, w_in, conv_w, pool_w, pool_scale, mem_norm_g, w_mem_kv,
           w_br_conv, w_br_pool, w_br_att, w_out, final_norm_g):
    f = lambda a: np.ascontiguousarray(np.asarray(a), dtype=np.float32)
    if "nc" not in _CACHE:
        _CACHE["nc"] = build_program(build_program(None))
    nc = _CACHE["nc"]
    vecs = _pack_vecs(norm_g, conv_w, pool_scale, mem_norm_g, final_norm_g)
    ident = np.eye(128, dtype=np.float32)
    shared = {"w_in": f(w_in), "pool_w": f(pool_w), "w_kv": f(w_mem_kv), "w_bc": f(w_br_conv), "w_bp": f(w_br_pool),
              "w_ba": f(w_br_att), "w_o": f(w_out), "vecs": vecs, "ident": ident}
    x_prompt, x_sample, mem_prompt = f(x_prompt), f(x_sample), f(mem_prompt)
    cache_mem_k, cache_mem_v, state_conv, state_pool = f(cache_mem_k), f(cache_mem_v), f(state_conv), f(state_pool)
    in_maps = []
    for c in range(8):
        bs = slice(16 * c, 16 * c + 16)
        m = dict(shared)
        m["x_p"] = x_prompt[c]
        m["x_s"] = f(x_sample[bs].transpose(1, 0, 2).reshape(64, D))
        m["mem"] = mem_prompt[c]
        m["ck"] = f(cache_mem_k[:, bs])
        m["cv"] = f(cache_mem_v[:, bs])
        m["st_c"] = f(state_conv[:, bs].transpose(0, 2, 1, 3).reshape(2, 32, D))
        m["st_p"] = f(state_pool[:, bs].transpose(0, 2, 1, 3).reshape(2, 240, D))
        in_maps.append(m)
    res = run_bass_kernel_spmd(nc, in_maps, core_ids=list(range(8)))
    R = res.results
    y_prompt = np.stack([R[c]["y_p"] for c in range(8)], 0)
    y_sample = np.concatenate([R[c]["y_s"].reshape(4, 16, D).transpose(1, 0, 2) for c in range(8)], 0)
    mk = np.stack([R[c]["o_mk"].reshape(2, NMEM, 4, 256) for c in range(8)], 1)
    mv = np.stack([R[c]["o_mv"].reshape(2, NMEM, 4, 256) for c in range(8)], 1)
    cvp = np.stack([R[c]["o_cvp"] for c in range(8)], 1)
    plp = np.stack([R[c]["o_plp"] for c in range(8)], 1)
    cvs = np.concatenate([R[c]["o_cvs"].reshape(2, 2, 16, D).transpose(0, 2, 1, 3) for c in range(8)], 1)
    pls = np.concatenate([R[c]["o_pls"].reshape(2, 15, 16, D).transpose(0, 2, 1, 3) for c in range(8)], 1)
    out = (y_prompt, y_sample, mk, mv, cvp, plp, cvs, pls)
    return tuple(np.ascontiguousarray(o, dtype=np.float32) for o in out)
```

```python
import numpy as np
from contextlib import ExitStack
import concourse.bass as bass
import concourse.mybir as mybir
from concourse.bass_utils import run_bass_kernel_spmd

F32 = mybir.dt.float32
BF16 = mybir.dt.bfloat16
AF = mybir.ActivationFunctionType
ALU = mybir.AluOpType

D = 1024
KC = 8
SEQ = 2048
NMEM = 256
DIN = 11264
EPS = 1e-6
NT = 1088
RING = 7
NSLOT = 15
SLOTW = 528
WIN = (2, 4, 8, 16)
STOP_AFTER = None

OFF_HC, OFF_BC, OFF_CC, OFF_GC, OFF_HP, OFF_GP, OFF_Q, OFF_GA, OFF_MC, OFF_MP, OFF_MA = [i * 1024 for i in range(11)]

V_NG, V_CW, V_PS, V_MG, V_FG, V_RC = 0, 16, 64, 80, 96, 104
NV = 104 + 60


class _Op:
    __slots__ = ("eng", "fn", "deps", "inc", "val", "dma_key", "dma_val", "name")

    def __init__(self, eng, fn, dma_key, name):
        self.eng = eng
        self.fn = fn
        self.deps = []
        self.inc = False
        self.val = None
        self.dma_key = dma_key
        self.dma_val = None
        self.name = name


class _Seg:
    __slots__ = ("lo", "hi", "writers", "readers", "prev_readers")

    def __init__(self, lo, hi, writers=None, readers=None, prev_readers=None):
        self.lo, self.hi = lo, hi
        self.writers = list(writers) if writers else []
        self.readers = list(readers) if readers else []
        self.prev_readers = list(prev_readers) if prev_readers else []

    def split(self, lo, hi):
        return _Seg(lo, hi, self.writers, self.readers, self.prev_readers)


BIG = 1 << 30


class Tracker:
    ENGS = ("pe", "act", "dve", "pool", "sp")

    def __init__(self):
        self.streams = {e: [] for e in self.ENGS}
        self.segs = {}
        self.dma_cnt = {}

    def _touch(self, res):
        if isinstance(res, str):
            res = (res, 0, BIG)
        name, lo, hi = res
        assert lo < hi, res
        segs = self.segs.setdefault(name, [])
        new, out = [], []
        pos = lo
        for s in segs:
            if s.hi <= lo or s.lo >= hi:
                new.append(s)
                continue
            if s.lo < lo:
                new.append(s.split(s.lo, lo))
                s.lo = lo
            right = None
            if s.hi > hi:
                right = s.split(hi, s.hi)
                s.hi = hi
            if pos < s.lo:
                g = _Seg(pos, s.lo)
                new.append(g)
                out.append(g)
            new.append(s)
            out.append(s)
            pos = s.hi
            if right is not None:
                new.append(right)
        if pos < hi:
            g = _Seg(pos, hi)
            new.append(g)
            out.append(g)
        new.sort(key=lambda s: s.lo)
        self.segs[name] = new
        return out

    def op(self, eng, fn, reads=(), writes=(), extra=(), dma_key=None, name=""):
        o = _Op(eng, fn, dma_key, name)
        deps = {}

        def add(p, kind):
            if p is o or p is None:
                return
            if p.dma_key is None and p.eng == eng:
                if eng not in ("act", "dve", "pool"):
                    return
            deps[id(p)] = p

        reads, writes = list(reads), list(writes)
        bank_names = {r[0] for r in reads + writes if not isinstance(r, str) and r[0][0] == "B" and r[0][1:].isdigit()}
        if bank_names:
            reads = [r for r in reads if isinstance(r, str) or r[0] not in bank_names]
            writes = [r for r in writes if isinstance(r, str) or r[0] not in bank_names]
            for bn in bank_names:
                reads.append((bn, 0, BIG))
                writes.append((bn, 0, BIG))
        rsegs = [s for r in reads for s in self._touch(r)]
        wsegs = [s for r in writes for s in self._touch(r)]
        rsegs = [s for r in reads for s in self._touch(r)]
        for st in rsegs:
            for w in st.writers:
                add(w, "raw")
        for st in wsegs:
            if st.readers:
                st.prev_readers = st.readers
                st.readers = []
                st.writers = []
            for x in st.prev_readers:
                add(x, "war")
            for x in st.writers:
                add(x, "waw")
        for st in rsegs:
            st.readers.append(o)
        for st in wsegs:
            st.writers.append(o)
        for p in extra:
            add(p, "raw")
        for p in deps.values():
            if p.dma_key is None:
                p.inc = True
        o.deps = list(deps.values())
        if dma_key is not None:
            c = self.dma_cnt.get(dma_key, 0) + 1
            self.dma_cnt[dma_key] = c
            o.dma_val = 16 * c
        self.streams[eng].append(o)
        return o

    def finalize(self):
        for e in self.ENGS:
            c = 0
            for o in self.streams[e]:
                if o.dma_key is None and o.inc:
                    c += 1
                    o.val = c

    def emit(self, eng_name, eng, sems, dma_sems):
        waited = {}
        for o in self.streams[eng_name]:
            need = {}
            for p in o.deps:
                if p.dma_key is not None:
                    key, val, sem = ("d", p.dma_key), p.dma_val, dma_sems[p.dma_key]
                else:
                    key, val, sem = ("e", p.eng), p.val, sems[p.eng]
                if key not in need or need[key][0] < val:
                    need[key] = (val, sem)
            for key, (val, sem) in need.items():
                if waited.get(key, 0) >= val:
                    continue
                waited[key] = val
                eng.wait_ge(sem, val)
            ins = o.fn(eng)
            if o.dma_key is not None:
                ins.then_inc(dma_sems[o.dma_key], 16)
            elif o.inc:
                ins.then_inc(sems[o.eng], 1)


class Tile:
    def __init__(self, name, kind, col, n, su, pos0, half, row0):
        self.name, self.kind, self.col, self.n, self.su, self.pos0, self.half, self.row0 = (
            name, kind, col, n, su, pos0, half, row0)
        self.tl = n // su


HALVES = [
    [Tile("P0", "P", 0, 512, 1, 0, 0, 0), Tile("P1", "P", 512, 512, 1, 512, 0, 512), Tile("S", "S", 1024, 64, 16, None, 0, 0)],
    [Tile("P2", "P", 0, 512, 1, 1024, 1, 1024), Tile("P3", "P", 512, 512, 1, 1536, 1, 1536)],
]


def build_program(order=None):
    dry = order is None
    nc = bass.Bass("TRN2", target_bir_lowering=False)
    dt = nc.dram_tensor
    x_p = dt("x_p", [SEQ, D], F32, kind="ExternalInput").ap()
    x_s = dt("x_s", [64, D], F32, kind="ExternalInput").ap()
    mem = dt("mem", [NMEM, D], F32, kind="ExternalInput").ap()
    ck = dt("ck", [2, 16, NMEM, 4, 256], F32, kind="ExternalInput").ap()
    cv = dt("cv", [2, 16, NMEM, 4, 256], F32, kind="ExternalInput").ap()
    st_c = dt("st_c", [2, 32, D], F32, kind="ExternalInput").ap()
    st_p = dt("st_p", [2, 240, D], F32, kind="ExternalInput").ap()
    w_in = dt("w_in", [2, D, DIN], F32, kind="ExternalInput").ap()
    pool_w = dt("pool_w", [2, 4, 256, 256], F32, kind="ExternalInput").ap()
    w_kv = dt("w_kv", [2, D, 2048], F32, kind="ExternalInput").ap()
    w_bc = dt("w_bc", [2, D, D], F32, kind="ExternalInput").ap()
    w_bp = dt("w_bp", [2, D, D], F32, kind="ExternalInput").ap()
    w_ba = dt("w_ba", [2, D, D], F32, kind="ExternalInput").ap()
    w_o = dt("w_o", [2, D, D], F32, kind="ExternalInput").ap()
    vecs = dt("vecs", [128, NV], F32, kind="ExternalInput").ap()
    ident_in = dt("ident", [128, 128], F32, kind="ExternalInput").ap()

    y_p = dt("y_p", [SEQ, D], F32, kind="ExternalOutput").ap()
    y_s = dt("y_s", [64, D], F32, kind="ExternalOutput").ap()
    o_mk = dt("o_mk", [2, NMEM, D], F32, kind="ExternalOutput").ap()
    o_mv = dt("o_mv", [2, NMEM, D], F32, kind="ExternalOutput").ap()
    o_cvp = dt("o_cvp", [2, 2, D], F32, kind="ExternalOutput").ap()
    o_plp = dt("o_plp", [2, 15, D], F32, kind="ExternalOutput").ap()
    o_cvs = dt("o_cvs", [2, 32, D], F32, kind="ExternalOutput").ap()
    o_pls = dt("o_pls", [2, 240, D], F32, kind="ExternalOutput").ap()

    T = Tracker()
    es = ExitStack()
    with es:
        sb = lambda name, shape, dtype: es.enter_context(nc.sbuf_tensor(name, shape, dtype))
        xT = sb("xT", [128, KC, NT], F32)
        hT = sb("hT", [128, KC, NT], BF16)
        aC = sb("aC", [128, KC, NT], BF16)
        aP = sb("aP", [128, KC, NT], BF16)
        aA = sb("aA", [128, KC, NT], BF16)
        mgR = sb("mgR", [128, KC * NT], BF16)
        wring = sb("wring", [128, RING, KC, 256], BF16)
        KTp = sb("KTp", [128, 2, KC, NMEM], BF16)
        Vp = sb("Vp", [128, 2, 2, D], BF16)
        scr = sb("scr", [128, NSLOT * SLOTW], F32)
        slots = [scr[:, i * SLOTW:(i + 1) * SLOTW] for i in range(NSLOT)]
        vt = sb("vt", [128, NV], F32)
        ident = sb("identf", [128, 128], F32)
        identb = sb("identb", [128, 128], BF16)
        ones_m = sb("ones_m", [128, 128], BF16)
        ones_1 = sb("ones_1", [128, 128], BF16)
        uH = sb("uH", [128, 2, KC, 2], F32)
        hpH = sb("hpH", [128, 2, KC, 15], F32)
        ucS = sb("ucS", [128, KC, 32], F32)
        cvSo = sb("cvSo", [128, KC, 32], F32)
        hpSo = sb("hpSo", [128, KC, 64], F32)
        tmp15 = sb("tmp15", [128, 2, 16], F32)
        qsbS = sb("qsbS", [128, KC, 64], BF16)
        sgaS = sb("sgaS", [128, KC, 64], F32)
        doS = sb("doS", [128, 4, 3, 64], F32)
        rdS = sb("rdS", [128, 64], F32)
        pqS = sb("pqS", [128, 2, 32], BF16)
        banks = [es.enter_context(nc.psum_tensor("bank%d" % i, [128, 512], F32)) for i in range(8)]

        mg = mgR[:, :].rearrange("p (c t) -> p c t", c=KC)
        Kq = [mgR[:, i * 2048:(i + 1) * 2048].rearrange("p (b m d) -> p b m d", b=4, m=2) for i in range(2)]
        Vq = [mgR[:, 4096 + i * 2048:4096 + (i + 1) * 2048].rearrange("p (b m d) -> p b m d", b=4, m=2) for i in range(2)]
        KTq = scr[:, 13 * SLOTW:13 * SLOTW + 1024].bitcast(BF16).rearrange("p (b c m) -> p b c m", b=4, c=2)

        free_banks = list(range(8))

        def balloc():
            assert free_banks, "PSUM bank allocator exhausted (record order needs > 8 live banks)"
            return free_banks.pop(0)

        def bfree(*bs):
            for b in bs:
                assert b not in free_banks
                free_banks.append(b)

        def rB(b, lo=0, hi=512):
            return ("B%d" % b, lo * 4, hi * 4)

        def rS(i, lo=0, hi=SLOTW):
            return ("S%d" % i, lo * 4, hi * 4)

        def rSb(i, lo, hi):
            return ("S%d" % i, lo * 2, hi * 2)

        def rW(r):
            return ("W", r * 4096, (r + 1) * 4096)

        def rX(c, col, n):
            return ("xT", (c * NT + col) * 4, (c * NT + col + n) * 4)

        def rA(name, c, col, n):
            return (name, (c * NT + col) * 2, (c * NT + col + n) * 2)

        def rKT(l, c):
            return ("KTp", ((l * 8 + c) * 256) * 2, ((l * 8 + c + 1) * 256) * 2)

        def rV(l, mc, col0, n):
            return ("Vp", ((l * 2 + mc) * 1024 + col0) * 2, ((l * 2 + mc) * 1024 + col0 + n) * 2)

        def rG(name, idx, w):
            return (name, idx * w * 4, (idx + 1) * w * 4)

        R_KQ = [("mgR", i * 4096, (i + 1) * 4096) for i in range(2)]
        R_VQ = [("mgR", 8192 + i * 4096, 8192 + (i + 1) * 4096) for i in range(2)]

        def rKTq(hb=None):
            if hb is None:
                return [rS(13), ("S14", 0, 4096 - SLOTW * 4)]
            if hb == 0:
                return [("S13", 0, 2048)]
            return [("S13", 2048, SLOTW * 4), ("S14", 0, 4096 - SLOTW * 4)]

        def sview(i, n, dtype=F32, off=0):
            if dtype == F32:
                return slots[i][:, off:off + n]
            return slots[i][:, :].bitcast(BF16)[:, off:off + n]

        def vcol(base, idx):
            return vt[:, base + idx: base + idx + 1]

        def mm(out_ap, pairs, reads, writes, name="mm"):
            pairs = list(pairs)

            def fn(pe):
                n = len(pairs)
                ins = None
                for i, (l, r) in enumerate(pairs):
                    ins = pe.matmul(out_ap, l, r, start=(i == 0), stop=(i == n - 1))
                return ins
            return T.op("pe", fn, reads, writes, name=name)

        def mm_multi(groups, reads, writes, name="mmm"):
            groups = [(o, list(p)) for o, p in groups]

            def fn(pe):
                ins = None
                for out_ap, pairs in groups:
                    n = len(pairs)
                    for i, (l, r) in enumerate(pairs):
                        ins = pe.matmul(out_ap, l, r, start=(i == 0), stop=(i == n - 1))
                return ins
            return T.op("pe", fn, reads, writes, name=name)

        def tr_multi(items, reads, writes, name="tr"):
            items = list(items)

            def fn(pe):
                ins = None
                for o, i, idn in items:
                    ins = pe.transpose(o, i, idn)
                return ins
            return T.op("pe", fn, reads, writes, name=name)

        def act(out, in_, func, reads, writes, scale=None, bias=None, name="act"):
            def fn(a):
                kw = {}
                if scale is not None:
                    kw["scale"] = scale
                if bias is not None:
                    kw["bias"] = bias
                return a.activation(out=out, in_=in_, func=func, **kw)
            return T.op("act", fn, reads, writes, name=name)

        def rsqrt_eps(b, n, slot, name):
            act(sview(slot, n), banks[b][:, 0:n], AF.Sqrt, [rB(b, 0, n)], [rS(slot, 0, n)], bias=EPS, name=name + "_sqrt")
            ew("dve", "recip", [rS(slot, 0, n)], [rS(slot, 0, n)], out=sview(slot, n), in_=sview(slot, n), name=name)

        def ew(eng, kind, reads, writes, name="ew", **kw):
            def fn(e):
                if kind == "tt":
                    return e.tensor_tensor(out=kw["out"], in0=kw["in0"], in1=kw["in1"], op=kw["op"])
                if kind == "ts":
                    if "op1" in kw:
                        return e.tensor_scalar(out=kw["out"], in0=kw["in0"], scalar1=kw["s1"], scalar2=kw["s2"],
                                               op0=kw["op0"], op1=kw["op1"])
                    return e.tensor_scalar(out=kw["out"], in0=kw["in0"], scalar1=kw["s1"], scalar2=None, op0=kw["op0"])
                if kind == "stt":
                    return e.scalar_tensor_tensor(out=kw["out"], in0=kw["in0"], scalar=kw["scalar"], in1=kw["in1"],
                                                  op0=kw["op0"], op1=kw["op1"])
                if kind == "copy":
                    return e.tensor_copy(kw["out"], kw["in_"])
                if kind == "recip":
                    return e.reciprocal(kw["out"], kw["in_"])
                if kind == "memset":
                    return e.memset(kw["out"], kw["value"])
                raise ValueError(kind)
            return T.op(eng, fn, reads, writes, name=name)

        def dma(queue, out, in_, key, reads, writes, name="dma"):
            def fn(q):
                return q.dma_start(out=out, in_=in_)
            return T.op(queue, fn, reads, writes, dma_key=key, name=name)

        def evac(which, out, in_, reads, writes, name="ev"):
            if which == 0:
                return act(out, in_, AF.Copy, reads, writes, name=name)
            return ew("dve", "copy", reads, writes, out=out, in_=in_, name=name)

        def wsrc(ap2d, nkc):
            return ap2d.rearrange("(kc p) n -> p kc n", p=128), nkc

        def block_schedule(l, half):
            out = []
            if half == 0:
                for J in range(4):
                    out.append(("kvK%d" % J, wsrc(w_kv[l, :, J * 256:(J + 1) * 256], 8)))
                for J in range(4):
                    out.append(("kvV%d" % J, wsrc(w_kv[l, :, 1024 + J * 256:1024 + (J + 1) * 256], 8)))
                for h in range(4):
                    out.append(("sq%d" % h, wsrc(w_in[l, :, OFF_Q + h * 256: OFF_Q + (h + 1) * 256], 8)))
                    out.append(("sga%d" % h, wsrc(w_in[l, :, OFF_GA + h * 256: OFF_GA + (h + 1) * 256], 8)))
            for J in range(4):
                for nm, off in (("hc", OFF_HC), ("cc", OFF_CC), ("bc", OFF_BC), ("gc", OFF_GC)):
                    out.append(("%s%d" % (nm, J), wsrc(w_in[l, :, off + J * 256: off + (J + 1) * 256], 8)))
            for g in range(4):
                out.append(("hp%d" % g, wsrc(w_in[l, :, OFF_HP + g * 256: OFF_HP + (g + 1) * 256], 8)))
                out.append(("gp%d" % g, wsrc(w_in[l, :, OFF_GP + g * 256: OFF_GP + (g + 1) * 256], 8)))
                out.append(("pw%d" % g, wsrc(pool_w[l, g], 2)))
            for h in range(4):
                out.append(("q%d" % h, wsrc(w_in[l, :, OFF_Q + h * 256: OFF_Q + (h + 1) * 256], 8)))
                out.append(("ga%d" % h, wsrc(w_in[l, :, OFF_GA + h * 256: OFF_GA + (h + 1) * 256], 8)))
            for J in range(4):
                for nm, wb, off in (("c", w_bc, OFF_MC), ("p", w_bp, OFF_MP), ("a", w_ba, OFF_MA)):
                    out.append(("wb%s%d" % (nm, J), wsrc(wb[l, :, J * 256:(J + 1) * 256], 8)))
                    out.append(("m%s%d" % (nm, J), wsrc(w_in[l, :, off + J * 256: off + (J + 1) * 256], 8)))
            for J in range(4):
                out.append(("wo%d" % J, wsrc(w_o[l, :, J * 256:(J + 1) * 256], 8)))
            return [("L%dH%d_%s" % (l, half, t), s) for t, s in out]

        src_map = {}
        for l in range(2):
            for tag, s in block_schedule(l, 0):
                src_map[(l, tag.split("_", 1)[1])] = s
        cur = {"l": 0}
        dry_order = []
        sched = [] if dry else [("L%d_%s" % (l_, sfx), src_map[(l_, sfx)]) for (l_, sfx) in order]
        ws = {"issued": 0, "taken": 0, "released": 0}

        ws_rel = [False] * len(sched)
        ws_taken_idx = []

        def ws_pump():
            if dry:
                return
            while ws["issued"] < len(sched):
                i = ws["issued"]
                if i >= RING and not ws_rel[i - RING]:
                    break
                tag, (src, nkc) = sched[i]
                r = i % RING
                dma("pool", wring[:, r, 0:nkc, :], src, "W%d" % r, [], [rW(r)], name="w_" + tag)
                ws["issued"] += 1

        def ws_take(tag_suffix, lay=None, want_idx=False):
            i = ws["taken"]
            lay = cur["l"] if lay is None else lay
            ws["taken"] += 1
            if dry:
                dry_order.append((lay, tag_suffix))
                return (i % RING, i) if want_idx else i % RING
            tag = sched[i][0]
            assert tag == "L%d_%s" % (lay, tag_suffix), (tag, lay, tag_suffix)
            assert i < ws["issued"], "weight block not prefetched (ring too small for phase)"
            ws_taken_idx.append(i)
            return (i % RING, i) if want_idx else i % RING

        def ws_release(n=1):
            if dry:
                return
            for _ in range(n):
                i = ws_taken_idx.pop(0)
                ws_rel[i] = True
            ws_pump()

        def ws_release_idx(i):
            if dry:
                return
            ws_taken_idx.remove(i)
            ws_rel[i] = True
            ws_pump()

        def setup():
            dma("sp", vt[:, :], vecs[:, :], "vt", [], ["vt"], name="ld_vecs")
            dma("sp", ident[:, :], ident_in[:, :], "ident", [], ["ident"], name="ld_ident")
            ew("dve", "copy", ["ident"], ["identb"], out=identb[:, :], in_=ident[:, :])
            ew("dve", "memset", [], ["ones_m"], out=ones_m[:, :], value=1.0 / 1024.0)
            ew("dve", "memset", [], ["ones_1"], out=ones_1[:, :], value=1.0)
            ws_pump()

        def load_rows(src_rows, ntok, sl0):
            outs = []
            for hb in range(2):
                sl = sl0 + hb
                dma("sp", slots[sl][0:ntok, 0:512], src_rows[:, hb * 512:(hb + 1) * 512], "S%d" % sl, [], [rS(sl, 0, 512)], name="ld_rows")
                b = balloc()
                items = []
                for cc in range(4):
                    items.append((banks[b][:, cc * ntok:(cc + 1) * ntok],
                                  slots[sl][0:ntok, cc * 128:(cc + 1) * 128], ident[0:ntok, 0:ntok]))
                tr_multi(items, [rS(sl, 0, 512), "ident"], [rB(b, 0, 4 * ntok)], name="tr_in")
                outs.append(b)
            return outs

        def load_x(half):
            par = 0
            for t in HALVES[half]:
                nsub = t.n // 128 if t.kind == "P" else 1
                ntok = 128 if t.kind == "P" else 64
                for s in range(nsub):
                    src = x_p[t.row0 + s * 128: t.row0 + (s + 1) * 128, :] if t.kind == "P" else x_s[:, :]
                    bs = load_rows(src, ntok, 2 * par)
                    par = (par + 1) % 4
                    for hb, b in enumerate(bs):
                        c0 = t.col + s * ntok
                        outv = xT[:, 4 * hb:4 * hb + 4, c0:c0 + ntok]
                        inv = banks[b][:, 0:4 * ntok].rearrange("p (c t) -> p c t", c=4)
                        wr = [rX(c, c0, ntok) for c in range(4 * hb, 4 * hb + 4)]
                        evac(hb, outv, inv, [rB(b, 0, 4 * ntok)], wr, name="ev_x")
                        bfree(b)

        def rms_stats(src_chunk, n, sq_slots, rstd_slot, src_reads, presq=False):
            for c in range(KC):
                if presq:
                    break
                si = sq_slots[c // 2]
                off = (c % 2) * 512
                act(sview(si, n, BF16, off=off), src_chunk(c), AF.Square, [src_reads(c)], [rSb(si, off, off + n)], name="sq")
            b = balloc()
            pairs = [(ones_m[:, :], sview(sq_slots[c // 2], n, BF16, off=(c % 2) * 512)) for c in range(KC)]
            rd = [rSb(sq_slots[c // 2], (c % 2) * 512, (c % 2) * 512 + n) for c in range(KC)]
            mm(banks[b][:, 0:n], pairs, rd + ["ones_m"], [rB(b, 0, n)], name="ss")
            rsqrt_eps(b, n, rstd_slot, "rstd")
            bfree(b)

        def presq_ok(l_prev, half):
            return l_prev >= 0 and not (half == 0 and l_prev == 0)

        def norm_phase(l, half):
            for ti, t in enumerate(HALVES[half]):
                p = ti % 2
                sq = [4 * p + k for k in range(4)]
                rs = 8 + p
                rms_stats(lambda c: xT[:, c, t.col:t.col + t.n], t.n, sq, rs, lambda c: rX(c, t.col, t.n),
                          presq=(t.kind == "P" and presq_ok(l - 1, half)))
                for c in range(KC):
                    ew("dve", "stt", [rX(c, t.col, t.n), rS(rs, 0, t.n), "vt"], [rA("hT", c, t.col, t.n)],
                       out=hT[:, c, t.col:t.col + t.n], in0=xT[:, c, t.col:t.col + t.n],
                       scalar=vcol(V_NG, l * 8 + c), in1=sview(rs, t.n), op0=ALU.mult, op1=ALU.mult, name="h")

        def kv_steps(l):
            steps = []

            def memT(c):
                return slots[10 + c // 2][:, (c % 2) * 256:(c % 2) * 256 + 256]

            def r_memT(c):
                return rS(10 + c // 2, (c % 2) * 256, (c % 2) * 256 + 256)

            def memn(c):
                return sview(4 + c // 4, 256, BF16, off=(c % 4) * 256)

            def r_memn(c):
                return rSb(4 + c // 4, (c % 4) * 256, (c % 4) * 256 + 256)
            mn_reads = [r_memn(c) for c in range(KC)]

            def stg(mc, col0):
                sl = 10 + 2 * mc + col0 // 512
                o = col0 % 512
                return slots[sl][:, o:o + 256], rS(sl, o, o + 256)

            def st_load(s):
                bs = load_rows(mem[s * 128:(s + 1) * 128, :], 128, 6)
                for hb, b in enumerate(bs):
                    for c4 in range(2):
                        sl = 10 + 2 * hb + c4
                        outv = slots[sl][:, 0:512].rearrange("p (c t) -> p c t", c=2)[:, :, s * 128:(s + 1) * 128]
                        inv = banks[b][:, c4 * 256:(c4 + 1) * 256].rearrange("p (c t) -> p c t", c=2)
                        wr = [rS(sl, cl * 256 + s * 128, cl * 256 + (s + 1) * 128) for cl in range(2)]
                        evac(hb, outv, inv, [rB(b, c4 * 256, (c4 + 1) * 256)], wr, name="ev_mem")
                    bfree(b)

            def st_stats():
                for c in range(KC):
                    si, off = c // 2, (c % 2) * 512
                    act(sview(si, 256, BF16, off=off), memT(c), AF.Square, [r_memT(c)], [rSb(si, off, off + 256)], name="sqm")
                b = balloc()
                pairs = [(ones_m[:, :], sview(c // 2, 256, BF16, off=(c % 2) * 512)) for c in range(KC)]
                mm(banks[b][:, 0:256], pairs, [rSb(c // 2, (c % 2) * 512, (c % 2) * 512 + 256) for c in range(KC)] + ["ones_m"],
                   [rB(b, 0, 256)], name="ssm")
                rsqrt_eps(b, 256, 8, "rstdm")
                bfree(b)

            def st_memn():
                for c in range(KC):
                    ew("dve", "stt", [r_memT(c), rS(8, 0, 256), "vt"], [r_memn(c)],
                       out=memn(c), in0=memT(c), scalar=vcol(V_MG, l * 8 + c), in1=sview(8, 256),
                       op0=ALU.mult, op1=ALU.mult, name="memn")

            def st_K(J):
                r, ri = ws_take("kvK%d" % J, lay=l, want_idx=True)
                for jj in range(2):
                    b = balloc()
                    mm(banks[b][:, 0:256],
                       [(wring[:, r, kc, jj * 128:(jj + 1) * 128], memn(kc)) for kc in range(KC)],
                       mn_reads + [rW(r)], [rB(b, 0, 256)], name="kT")
                    act(KTp[:, l, 2 * J + jj, :], banks[b][:, 0:256], AF.Copy, [rB(b, 0, 256)], [rKT(l, 2 * J + jj)], name="ev_kT")
                    bfree(b)
                for mc in range(2):
                    b = balloc()
                    mm(banks[b][:, 0:256],
                       [(memn(kc)[:, mc * 128:(mc + 1) * 128], wring[:, r, kc, :]) for kc in range(KC)],
                       mn_reads + [rW(r)], [rB(b, 0, 256)], name="ktok")
                    o, ro = stg(mc, J * 256)
                    ew("dve", "copy", [rB(b, 0, 256)], [ro], out=o, in_=banks[b][:, 0:256], name="ev_ktok")
                    bfree(b)
                ws_release_idx(ri)

            def st_store(dst, nm):
                for mc in range(2):
                    for hf in range(2):
                        sl = 10 + 2 * mc + hf
                        dma("sp", dst[l, mc * 128:(mc + 1) * 128, hf * 512:(hf + 1) * 512], slots[sl][:, 0:512], "S%d" % sl,
                            [rS(sl, 0, 512)], [], name=nm)

            def st_V(J):
                r, ri = ws_take("kvV%d" % J, lay=l, want_idx=True)
                for mc in range(2):
                    b = balloc()
                    mm(banks[b][:, 0:256],
                       [(memn(kc)[:, mc * 128:(mc + 1) * 128], wring[:, r, kc, :]) for kc in range(KC)],
                       mn_reads + [rW(r)], [rB(b, 0, 256)], name="vtok")
                    act(Vp[:, l, mc, J * 256:(J + 1) * 256], banks[b][:, 0:256], AF.Copy, [rB(b, 0, 256)], [rV(l, mc, J * 256, 256)], name="ev_v")
                    o, ro = stg(mc, J * 256)
                    ew("dve", "copy", [rB(b, 0, 256)], [ro], out=o, in_=banks[b][:, 0:256], name="ev_vtok")
                    bfree(b)
                ws_release_idx(ri)

            steps.append(lambda: st_load(0))
            steps.append(lambda: st_load(1))
            steps.append(st_stats)
            steps.append(st_memn)
            for J in range(4):
                steps.append(lambda J=J: st_K(J))
            steps.append(lambda: st_store(o_mk, "st_mk"))
            for J in range(4):
                steps.append(lambda J=J: st_V(J))
            steps.append(lambda: st_store(o_mv, "st_mv"))
            return steps

        def kv_phase(l):
            for s in kv_steps(l):
                s()

        def store_rows(src_chunk, ncol, dst_rows, src_reads, sl0):
            for hb in range(2):
                b = balloc()
                items = [(banks[b][0:ncol, cc * 128:(cc + 1) * 128], src_chunk(4 * hb + cc), ident[:, :]) for cc in range(4)]
                rds = ["ident"] + [src_reads(4 * hb + cc) for cc in range(4)]
                tr_multi(items, rds, [rB(b)], name="tr_out")
                sl = sl0 + hb
                evac(hb, slots[sl][0:ncol, 0:512], banks[b][0:ncol, 0:512], [rB(b)], [rS(sl, 0, 512)], name="ev_out")
                bfree(b)
                dma("sp", dst_rows[:, hb * 512:(hb + 1) * 512], slots[sl][0:ncol, 0:512], "S%d" % sl, [rS(sl, 0, 512)], [], name="st_rows")

        bg = []
        bg_heads = set()

        def bg_drain(k):
            cnt = 0
            while bg and (k is None or cnt < k) and bg[0][0] in bg_heads:
                bg.pop(0)[1]()
                cnt += 1
            if k is None:
                assert not bg

        def sample_pre_att(l):
            t = HALVES[0][2]
            n = t.n
            hrd = [rA("hT", kc, t.col, n) for kc in range(KC)]
            rhs = [hT[:, kc, t.col:t.col + n] for kc in range(KC)]

            def st_LK(u):
                hh, qd, kb = u // 4, u % 4, u % 2
                ksrc = ck[l, 4 * qd:4 * qd + 4, :, hh, :].rearrange("b (m p) d -> p b m d", p=128)
                dma("pool", Kq[kb], ksrc, "Kq%d" % kb, [], [R_KQ[kb]], name="ld_K")

            def st_LV(u):
                hh, qd, kb = u // 4, u % 4, u % 2
                vsrc = cv[l, 4 * qd:4 * qd + 4, :, hh, :].rearrange("b (m p) d -> p b m d", p=128)
                dma("pool", Vq[kb], vsrc, "Vq%d" % kb, [], [R_VQ[kb]], name="ld_V")
            st_LK(0)
            st_LK(1)
            bg_heads.clear()
            bg_heads.add(-1)

            def head(hh, taken):
                (rq, iq), (rga, iga) = taken
                for dc in range(2):
                    c = 2 * hh + dc
                    cs = slice(dc * 128, (dc + 1) * 128)
                    b = balloc()
                    mm(banks[b][:, 0:n], [(wring[:, rq, kc, cs], rhs[kc]) for kc in range(KC)],
                       hrd + [rW(rq)], [rB(b, 0, n)], name="mm_sq")
                    act(qsbS[:, c, :], banks[b][:, 0:n], AF.Copy, [rB(b, 0, n)], [rG("qsbS", c, 32)], scale=1.0 / 16.0, name="ev_sq")
                    bfree(b)
                    b = balloc()
                    mm(banks[b][:, 0:n], [(wring[:, rga, kc, cs], rhs[kc]) for kc in range(KC)],
                       hrd + [rW(rga)], [rB(b, 0, n)], name="mm_sga")
                    act(sgaS[:, c, :], banks[b][:, 0:n], AF.Tanh, [rB(b, 0, n)], [rG("sgaS", c, 64)], scale=0.5, name="tanh_sga")
                    ew("dve", "stt", [rG("sgaS", c, 64), rB(b, 0, n)], [rG("sgaS", c, 64)], out=sgaS[:, c, :],
                       in0=sgaS[:, c, :], scalar=1.0, in1=banks[b][:, 0:n], op0=ALU.add, op1=ALU.mult, name="silu2_s")
                    bfree(b)
                ws_release_idx(iq)
                ws_release_idx(iga)
                bg_heads.add(hh)

            def st_T(u):
                kb = u % 2
                for hb in range(2):
                    bb = balloc()
                    items_ = []
                    bv = banks[bb][:, :].bitcast(BF16)
                    for bl in range(2):
                        for dc in range(2):
                            for mc in range(2):
                                c0 = (bl * 2 + dc) * 256 + mc * 128
                                items_.append((bv[:, c0:c0 + 128], Kq[kb][:, 2 * hb + bl, mc, dc * 128:(dc + 1) * 128], identb[:, :]))
                    tr_multi(items_, [R_KQ[kb], "identb"], [rB(bb)], name="tr_K")
                    outv = KTq[:, 2 * hb:2 * hb + 2, :, :]
                    inv = bv[:, 0:1024].rearrange("p (b c m) -> p b c m", b=2, c=2)
                    evac(0, outv, inv, [rB(bb)], rKTq(hb), name="ev_KT")
                    bfree(bb)

            def st_S(u):
                hh, qd, kb = u // 4, u % 4, u % 2
                b_s = balloc()
                groups = []
                for bl in range(4):
                    bgi = 4 * qd + bl
                    for mc in range(2):
                        c0 = bl * 8 + mc * 4
                        groups.append((banks[b_s][:, c0:c0 + 4],
                                       [(KTq[:, bl, dc, mc * 128:(mc + 1) * 128], qsbS[:, 2 * hh + dc, bgi:64:16]) for dc in range(2)]))
                mm_multi(groups, [rG("qsbS", 2 * hh, 32), rG("qsbS", 2 * hh + 1, 32)] + rKTq(), [rB(b_s, 0, 32)], name="mm_ss")
                act(pqS[:, kb, :], banks[b_s][:, 0:32], AF.Exp, [rB(b_s, 0, 32)], [rG("pqS", kb, 16)], name="exp_s")
                bfree(b_s)

            def st_O(u):
                hh, qd, kb = u // 4, u % 4, u % 2
                pq = pqS[:, kb, :]
                r_pq = rG("pqS", kb, 16)
                b = balloc()
                pq4 = pq.rearrange("p (b m t) -> p b m t", b=4, m=2)
                mm(banks[b][:, 0:16], [(ones_1[:, :], pq4[:, :, mc, :]) for mc in range(2)], [r_pq, "ones_1"], [rB(b, 0, 16)], name="mm_dens")
                for dc in range(2):
                    groups = []
                    for bl in range(4):
                        groups.append((banks[b][:, 16 + dc * 16 + bl * 4: 16 + dc * 16 + bl * 4 + 4],
                                       [(Vq[kb][:, bl, mc, dc * 128:(dc + 1) * 128], pq[:, bl * 8 + mc * 4: bl * 8 + mc * 4 + 4])
                                        for mc in range(2)]))
                    mm_multi(groups, [r_pq, R_VQ[kb]], [rB(b, 16 + dc * 16, 32 + dc * 16)], name="mm_os")
                outv = doS[:, hh, :, :].rearrange("p w (t b) -> p w b t", b=16)[:, :, 4 * qd:4 * qd + 4, :]
                inv = banks[b][:, 0:48].rearrange("p (w b t) -> p w b t", w=3, b=4)
                act(outv, inv, AF.Copy, [rB(b, 0, 48)], [rG("doS", hh, 192)], name="ev_do")
                bfree(b)

            def st_E(hh):
                ew("dve", "recip", [rG("doS", hh, 192)], ["rdS"], out=rdS[:, :], in_=doS[:, hh, 0, :], name="rden_s")
                for dc in range(2):
                    c = 2 * hh + dc
                    ew("dve", "stt", [rG("sgaS", c, 64), "rdS"], [rG("sgaS", c, 64)], out=sgaS[:, c, :],
                       in0=sgaS[:, c, :], scalar=0.5, in1=rdS[:, :], op0=ALU.mult, op1=ALU.mult, name="gg_s")
                    ew("dve", "tt", [rG("doS", hh, 192), rG("sgaS", c, 64)], [rA("aA", c, t.col, n)],
                       out=aA[:, c, t.col:t.col + n], in0=doS[:, hh, 1 + dc, :], in1=sgaS[:, c, :], op=ALU.mult, name="aA_s")

            def macro(k):
                if 0 <= k < 16:
                    st_S(k)
                if 0 <= k - 1 < 16:
                    st_O(k - 1)
                    if (k - 1) % 4 == 3:
                        st_E((k - 1) // 4)
                if 0 <= k + 1 < 16:
                    st_LV(k + 1)
                if k + 1 < 16:
                    st_T(k + 1)
                if k + 3 < 16:
                    st_LK(k + 3)
            for k in range(-1, 17):
                bg.append((min(max(k, -1), 15) // 4 if k >= 0 else -1, lambda k=k: macro(k)))
            return head

        def conv_phase(l, half):
            tiles = HALVES[half]
            if half == 0:
                bs = load_rows(st_c[l], 32, 10)
                for hb, b in enumerate(bs):
                    outv = ucS[:, 4 * hb:4 * hb + 4, :]
                    inv = banks[b][:, 0:128].rearrange("p (c t) -> p c t", c=4)
                    ew("dve", "copy", [rB(b, 0, 128)], [("ucS", hb * 512, (hb + 1) * 512)], out=outv, in_=inv, name="ev_ucS")
                    bfree(b)
            it = 0
            pcount = 0
            deferred = []
            spre_head = sample_pre_att(l) if half == 0 else None
            for J in range(4):
                rh, rc_, rb, rg = ws_take("hc%d" % J), ws_take("cc%d" % J), ws_take("bc%d" % J), ws_take("gc%d" % J)
                if half == 0 and J > 0:
                    taken = (ws_take("sq%d" % (J - 1), want_idx=True), ws_take("sga%d" % (J - 1), want_idx=True))
                    deferred.append(lambda Jp=J - 1, taken=taken: spre_head(Jp, taken))
                for jj in range(2):
                    j = 2 * J + jj
                    cs = slice(jj * 128, (jj + 1) * 128)
                    if J == 0 and jj == 0:
                        ctiles = [t for t in tiles if t.kind == "P"][:1] + [t for t in tiles if t.kind == "S"] + [t for t in tiles if t.kind == "P"][1:]
                    else:
                        ctiles = [t for t in tiles if t.kind == "S"] + [t for t in tiles if t.kind == "P"]
                    for ti, t in enumerate(ctiles):
                        last_item = (jj == 1 and ti == len(ctiles) - 1)
                        n, su = t.n, t.su
                        p = it % 2
                        it += 1
                        S_hc, S_cc, S_u, S_sg, S_bc = [5 * p + k for k in range(5)]
                        hrd = [rA("hT", kc, t.col, n) for kc in range(KC)]
                        rhs = [hT[:, kc, t.col:t.col + n] for kc in range(KC)]
                        b_h, b_c, b_b, b_g = balloc(), balloc(), balloc(), balloc()
                        for bb, r in ((b_h, rh), (b_c, rc_), (b_b, rb), (b_g, rg)):
                            mm(banks[bb][:, 0:n], [(wring[:, r, kc, cs], rhs[kc]) for kc in range(KC)],
                               hrd + [rW(r)], [rB(bb, 0, n)], name="mm_conv")
                            if last_item:
                                ws_release(1)
                        ue = slots[S_u]
                        r_head = rS(S_u, 0, 2 * su)
                        r_body = rS(S_u, 2 * su, 2 * su + n)
                        r_all = rS(S_u, 0, 2 * su + n)
                        if t.kind == "S":
                            ew("pool", "copy", [rG("ucS", j, 32)], [r_head], out=ue[:, 0:32], in_=ucS[:, j, :], name="halo_in")
                        elif t.pos0 == 0:
                            ew("pool", "memset", [], [r_head], out=ue[:, 0:2], value=0.0, name="halo_z")
                        else:
                            ew("pool", "copy", [rG("uH", l * 8 + j, 2)], [r_head], out=ue[:, 0:2], in_=uH[:, l, j, :], name="halo_in")
                        act(sview(S_hc, n), banks[b_h][:, 0:n], AF.Copy, [rB(b_h, 0, n)], [rS(S_hc, 0, n)], name="ev_hc")
                        act(sview(S_cc, n), banks[b_c][:, 0:n], AF.Copy, [rB(b_c, 0, n)], [rS(S_cc, 0, n)], name="ev_cc")
                        act(sview(S_bc, n), banks[b_b][:, 0:n], AF.Copy, [rB(b_b, 0, n)], [rS(S_bc, 0, n)], name="ev_bc")
                        act(sview(S_sg, n), banks[b_g][:, 0:n], AF.Silu, [rB(b_g, 0, n)], [rS(S_sg, 0, n)], name="silu_gc")
                        bfree(b_h, b_c, b_b, b_g)
                        ew("dve", "tt", [rS(S_cc, 0, n), rS(S_hc, 0, n)], [r_body], out=ue[:, 2 * su:2 * su + n],
                           in0=sview(S_cc, n), in1=sview(S_hc, n), op=ALU.mult, name="u")
                        if t.kind == "S":
                            ew("pool", "copy", [rS(S_u, n, n + 32)], [rG("cvSo", j, 32)], out=cvSo[:, j, :], in_=ue[:, n:n + 32], name="halo_out")
                        else:
                            ew("pool", "copy", [rS(S_u, n, n + 2)], [rG("uH", l * 8 + j, 2)], out=uH[:, l, j, :], in_=ue[:, n:n + 2], name="halo_out")
                        ya = sview(S_hc, n)
                        yb = sview(S_cc, n)
                        ew("dve", "ts", [r_all, "vt"], [rS(S_hc, 0, n)], out=ya, in0=ue[:, 0:n], s1=vcol(V_CW, l * 24 + 0 * 8 + j),
                           s2=None, op0=ALU.mult, name="y0")
                        ew("dve", "stt", [r_all, "vt", rS(S_hc, 0, n)], [rS(S_cc, 0, n)], out=yb, in0=ue[:, su:su + n],
                           scalar=vcol(V_CW, l * 24 + 1 * 8 + j), in1=ya, op0=ALU.mult, op1=ALU.add, name="y1")
                        ew("dve", "stt", [r_all, "vt", rS(S_cc, 0, n)], [rS(S_hc, 0, n)], out=ya, in0=ue[:, 2 * su:2 * su + n],
                           scalar=vcol(V_CW, l * 24 + 2 * 8 + j), in1=yb, op0=ALU.mult, op1=ALU.add, name="y2")
                        ew("dve", "tt", [rS(S_bc, 0, n), rS(S_sg, 0, n)], [rS(S_cc, 0, n)], out=yb, in0=sview(S_bc, n), in1=sview(S_sg, n),
                           op=ALU.mult, name="bsg")
                        ew("dve", "tt", [rS(S_hc, 0, n), rS(S_cc, 0, n)], [rA("aC", j, t.col, n)], out=aC[:, j, t.col:t.col + n],
                           in0=ya, in1=yb, op=ALU.mult, name="aC")
                        if t.kind == "P":
                            if deferred:
                                deferred.pop(0)()
                            pcount += 1
                            if pcount % 2 == 0:
                                bg_drain(1)
                if half == 0 and J == 3:
                    taken = (ws_take("sq%d" % J, want_idx=True), ws_take("sga%d" % J, want_idx=True))
                    deferred.append(lambda J=J, taken=taken: spre_head(J, taken))
            while deferred:
                deferred.pop(0)()
            if half == 0:
                store_rows(lambda c: cvSo[:, c, :], 32, o_cvs[l], lambda c: rG("cvSo", c, 32), 10)
            else:
                store_rows(lambda c: uH[:, l, c, :], 2, o_cvp[l], lambda c: rG("uH", l * 8 + c, 2), 10)

        def pool_phase(l, half):
            tiles = HALVES[half]
            if half == 0:
                dma("sp", o_pls[l, 0:176, :], st_p[l, 64:240, :], "d2d", [], [], name="d2d_pool")

            def state_dma(g):
                dma("sp", slots[12][0:128, 0:256], st_p[l, 0:128, g * 256:(g + 1) * 256], "S12", [], [rS(12, 0, 256)], name="ld_pst0")
                dma("sp", slots[12][0:112, 256:512], st_p[l, 128:240, g * 256:(g + 1) * 256], "S12", [], [rS(12, 256, 512)], name="ld_pst1")

            def state_to_halo(S_e):
                b = balloc()
                its = []
                for jj in range(2):
                    for rg, (r0, nr) in enumerate(((0, 128), (128, 112))):
                        its.append((banks[b][:, jj * 240 + r0: jj * 240 + r0 + nr],
                                    slots[12][0:nr, rg * 256 + jj * 128: rg * 256 + (jj + 1) * 128], ident[0:nr, 0:nr]))
                tr_multi(its, [rS(12, 0, 512), "ident"], [rB(b, 0, 480)], name="tr_pst")
                for jj in range(2):
                    act(slots[S_e[jj]][:, 0:240], banks[b][:, jj * 240:(jj + 1) * 240], AF.Copy, [rB(b, 0, 480)],
                        [rS(S_e[jj], 0, 240)], name="phalo_in_s")
                bfree(b)

            items = []
            for g in (3, 0, 2, 1):
                for ti, t in enumerate(tiles):
                    items.append(dict(g=g, t=t, first=(ti == 0), last=(ti == len(tiles) - 1), idx=len(items)))
            wslots = {}

            def stage_a(itm):
                g, t = itm["g"], itm["t"]
                if itm["first"]:
                    wslots[g] = (ws_take("hp%d" % g), ws_take("gp%d" % g), ws_take("pw%d" % g))
                    if half == 0:
                        state_dma(g)
                rhp, rgp, rpw = wslots[g]
                w, m = WIN[g], g + 1
                n, su = t.n, t.su
                L = 15 * su + n
                p = itm["idx"] % 2
                S_e = [0 + 2 * p, 1 + 2 * p]
                S_sg = [4 + 2 * p, 5 + 2 * p]
                S_mx = 8 + p
                S_tA, S_tB = 10, 11
                hrd = [rA("hT", kc, t.col, n) for kc in range(KC)]
                rhs = [hT[:, kc, t.col:t.col + n] for kc in range(KC)]
                b_hp = [balloc(), balloc()]
                b_gp = [balloc(), balloc()]
                for jj in range(2):
                    cs = slice(jj * 128, (jj + 1) * 128)
                    mm(banks[b_hp[jj]][:, 0:n], [(wring[:, rhp, kc, cs], rhs[kc]) for kc in range(KC)],
                       hrd + [rW(rhp)], [rB(b_hp[jj], 0, n)], name="mm_hp")
                    mm(banks[b_gp[jj]][:, 0:n], [(wring[:, rgp, kc, cs], rhs[kc]) for kc in range(KC)],
                       hrd + [rW(rgp)], [rB(b_gp[jj], 0, n)], name="mm_gp")
                mx = [sview(S_mx, n, BF16, off=jj * 512) for jj in range(2)]
                r_mx = [rSb(S_mx, jj * 512, jj * 512 + n) for jj in range(2)]
                itm.update(mx=mx, r_mx=r_mx, S_sg=S_sg, rpw=rpw)
                if t.kind == "S":
                    state_to_halo(S_e)
                for jj in range(2):
                    j = 2 * g + jj
                    e = slots[S_e[jj]]
                    r_head = rS(S_e[jj], 0, 15 * su)
                    r_body = rS(S_e[jj], 15 * su, L)
                    r_all = rS(S_e[jj], 0, L)
                    if t.kind == "S":
                        pass
                    elif t.pos0 == 0:
                        ew("pool", "memset", [], [r_head], out=e[:, 0:15], value=0.0, name="phalo_z")
                    else:
                        ew("pool", "copy", [rG("hpH", l * 8 + j, 15)], [r_head], out=e[:, 0:15], in_=hpH[:, l, j, :], name="phalo_in")
                    act(e[:, 15 * su:L], banks[b_hp[jj]][:, 0:n], AF.Copy, [rB(b_hp[jj], 0, n)], [r_body], name="ev_hp")
                    if t.kind == "S":
                        ew("pool", "copy", [rS(S_e[jj], 240, 304)], [rG("hpSo", j, 64)], out=hpSo[:, j, :], in_=e[:, 240:304], name="phalo_out")
                    else:
                        ew("pool", "copy", [rS(S_e[jj], n, n + 15)], [rG("hpH", l * 8 + j, 15)], out=hpH[:, l, j, :], in_=e[:, n:n + 15], name="phalo_out")
                    act(sview(S_sg[jj], n), banks[b_gp[jj]][:, 0:n], AF.Silu, [rB(b_gp[jj], 0, n)], [rS(S_sg[jj], 0, n)], name="silu_gp")
                bfree(*b_hp)
                bfree(*b_gp)
                for jj in range(2):
                    e = slots[S_e[jj]]
                    r_all = rS(S_e[jj], 0, L)
                    lo = [0] * (m + 1)
                    lo[m] = 15
                    for k in range(m, 0, -1):
                        lo[k - 1] = lo[k] - (1 << (k - 1))
                    cur, cur_rd = e, r_all
                    tsl = [S_tA, S_tB]
                    for k in range(1, m + 1):
                        sh = (1 << (k - 1)) * su
                        a0 = lo[k] * su
                        dst_sl = tsl[k % 2]
                        dst = slots[dst_sl]
                        ew("dve", "tt", [cur_rd], [rS(dst_sl, a0, L)], out=dst[:, a0:L], in0=cur[:, a0:L], in1=cur[:, a0 - sh:L - sh],
                           op=ALU.add, name="win")
                        cur, cur_rd = dst, rS(dst_sl, 0, L)
                    ew("dve", "stt", [cur_rd, r_all], [r_mx[jj]], out=mx[jj], in0=cur[:, 15 * su:L], scalar=1.0 / w,
                       in1=e[:, 15 * su:L], op0=ALU.mult, op1=ALU.subtract, name="mixed")
                    if t.kind == "P" and t.pos0 == 0:
                        ew("dve", "tt", [cur_rd, "vt"], [rG("tmp15", jj, 16)], out=tmp15[:, jj, 0:15], in0=cur[:, 15:30],
                           in1=vt[:, V_RC + g * 15: V_RC + (g + 1) * 15], op=ALU.mult, name="fix1")
                        ew("dve", "tt", [rG("tmp15", jj, 16), r_all], [rSb(S_mx, jj * 512, jj * 512 + 15)], out=mx[jj][:, 0:15],
                           in0=tmp15[:, jj, 0:15], in1=e[:, 15:30], op=ALU.subtract, name="fix2")

            def stage_b(itm):
                g, t = itm["g"], itm["t"]
                n = t.n
                mx, r_mx, S_sg, rpw = itm["mx"], itm["r_mx"], itm["S_sg"], itm["rpw"]
                for dd in range(2):
                    b = balloc()
                    mm(banks[b][:, 0:n], [(wring[:, rpw, jj, dd * 128:(dd + 1) * 128], mx[jj]) for jj in range(2)],
                       r_mx + [rW(rpw)], [rB(b, 0, n)], name="mm_pool")
                    j = 2 * g + dd
                    ew("dve", "stt", [rB(b, 0, n), "vt", rS(S_sg[dd], 0, n)], [rA("aP", j, t.col, n)], out=aP[:, j, t.col:t.col + n],
                       in0=banks[b][:, 0:n], scalar=vcol(V_PS, l * 8 + j), in1=sview(S_sg[dd], n),
                       op0=ALU.mult, op1=ALU.mult, name="aP")
                    bfree(b)
                if itm["last"]:
                    ws_release(3)
                if t.kind == "P":
                    bg_drain(1)

            for k in range(len(items) + 1):
                if k < len(items):
                    stage_a(items[k])
                if k >= 1:
                    stage_b(items[k - 1])
            if half == 0:
                store_rows(lambda c: hpSo[:, c, :], 64, o_pls[l, 176:240, :], lambda c: rG("hpSo", c, 64), 10)
            else:
                store_rows(lambda c: hpH[:, l, c, :], 15, o_plp[l], lambda c: rG("hpH", l * 8 + c, 15), 10)

        def att_phase(l, half):
            tiles = [t for t in HALVES[half] if t.kind == "P"]
            items = []
            for hh in range(4):
                for ti, t in enumerate(tiles):
                    items.append(dict(hh=hh, t=t, first=(ti == 0), last=(ti == len(tiles) - 1), idx=len(items)))
            wsl = {}

            def stage_a(itm):
                hh, t = itm["hh"], itm["t"]
                if itm["first"]:
                    wsl[hh] = (ws_take("q%d" % hh), ws_take("ga%d" % hh))
                rq, rga = wsl[hh]
                n = t.n
                i = itm["idx"]
                S_q = i % 2
                S_sga = [6 + 2 * (i % 3), 7 + 2 * (i % 3)]
                hrd = [rA("hT", kc, t.col, n) for kc in range(KC)]
                rhs = [hT[:, kc, t.col:t.col + n] for kc in range(KC)]
                qsb = [sview(S_q, n, BF16, off=dc * 512) for dc in range(2)]
                qrd = [rSb(S_q, dc * 512, dc * 512 + n) for dc in range(2)]
                itm.update(qsb=qsb, qrd=qrd, S_sga=S_sga, S_p=2 + i % 2, S_rd=4 + i % 2)
                for dc in range(2):
                    cs = slice(dc * 128, (dc + 1) * 128)
                    b = balloc()
                    mm(banks[b][:, 0:n], [(wring[:, rq, kc, cs], rhs[kc]) for kc in range(KC)],
                       hrd + [rW(rq)], [rB(b, 0, n)], name="mm_q")
                    act(qsb[dc], banks[b][:, 0:n], AF.Copy, [rB(b, 0, n)], [qrd[dc]], scale=1.0 / 16.0, name="ev_q")
                    bfree(b)
                for dc in range(2):
                    cs = slice(dc * 128, (dc + 1) * 128)
                    b = balloc()
                    mm(banks[b][:, 0:n], [(wring[:, rga, kc, cs], rhs[kc]) for kc in range(KC)],
                       hrd + [rW(rga)], [rB(b, 0, n)], name="mm_ga")
                    act(sview(S_sga[dc], n), banks[b][:, 0:n], AF.Tanh, [rB(b, 0, n)], [rS(S_sga[dc], 0, n)], scale=0.5, name="tanh_ga")
                    ew("dve", "stt", [rS(S_sga[dc], 0, n), rB(b, 0, n)], [rS(S_sga[dc], 0, n)], out=sview(S_sga[dc], n),
                       in0=sview(S_sga[dc], n), scalar=1.0, in1=banks[b][:, 0:n], op0=ALU.add, op1=ALU.mult, name="silu2")
                    bfree(b)
                if itm["last"]:
                    ws_release(2)
                bg_drain(1)

            def stage_b(itm):
                hh, t = itm["hh"], itm["t"]
                n = t.n
                qsb, qrd, S_p = itm["qsb"], itm["qrd"], itm["S_p"]
                if t.kind == "P":
                    pT = [sview(S_p, n, BF16, off=mc * 512) for mc in range(2)]
                    prd = [rSb(S_p, mc * 512, mc * 512 + n) for mc in range(2)]
                    itm.update(pT=pT, prd=prd)
                    for mc in range(2):
                        b = balloc()
                        mm(banks[b][:, 0:n],
                           [(KTp[:, l, 2 * hh + dc, mc * 128:(mc + 1) * 128], qsb[dc]) for dc in range(2)],
                           qrd + [rKT(l, 2 * hh), rKT(l, 2 * hh + 1)], [rB(b, 0, n)], name="mm_s")
                        act(pT[mc], banks[b][:, 0:n], AF.Exp, [rB(b, 0, n)], [prd[mc]], name="exp")
                        bfree(b)
                    return
                raise AssertionError("sample tile is handled by the background sample-attention steps")

            def stage_c(itm):
                hh, t = itm["hh"], itm["t"]
                n = t.n
                S_sga, S_rd = itm["S_sga"], itm["S_rd"]
                if t.kind == "P":
                    pT, prd = itm["pT"], itm["prd"]
                    b_den = balloc()
                    b_o = [balloc(), balloc()]
                    mm(banks[b_den][:, 0:n], [(ones_1[:, :], pT[mc]) for mc in range(2)], prd + ["ones_1"], [rB(b_den, 0, n)], name="mm_den")
                    for dc in range(2):
                        mm(banks[b_o[dc]][:, 0:n],
                           [(Vp[:, l, mc, hh * 256 + dc * 128: hh * 256 + (dc + 1) * 128], pT[mc]) for mc in range(2)],
                           prd + [rV(l, mc, hh * 256, 256) for mc in range(2)], [rB(b_o[dc], 0, n)], name="mm_o")
                ew("dve", "recip", [rB(b_den, 0, n)], [rS(S_rd, 0, n)], out=sview(S_rd, n), in_=banks[b_den][:, 0:n], name="rden")
                bfree(b_den)
                for dc in range(2):
                    ew("dve", "stt", [rS(S_sga[dc], 0, n), rS(S_rd, 0, n)], [rS(S_sga[dc], 0, n)], out=sview(S_sga[dc], n),
                       in0=sview(S_sga[dc], n), scalar=0.5, in1=sview(S_rd, n), op0=ALU.mult, op1=ALU.mult, name="gg")
                    ew("dve", "tt", [rB(b_o[dc], 0, n), rS(S_sga[dc], 0, n)], [rA("aA", 2 * hh + dc, t.col, n)],
                       out=aA[:, 2 * hh + dc, t.col:t.col + n], in0=banks[b_o[dc]][:, 0:n], in1=sview(S_sga[dc], n),
                       op=ALU.mult, name="aA")
                    bfree(b_o[dc])

            ni = len(items)
            for k in range(ni + 2):
                if k < ni:
                    stage_a(items[k])
                if 0 <= k - 1 < ni:
                    stage_b(items[k - 1])
                if 0 <= k - 2 < ni:
                    stage_c(items[k - 2])

        def merge_phase(l, half):
            bg_drain(None)
            tiles = HALVES[half]

            def acc(jj, t):
                sl = 6 * jj + t.col // 512
                return slots[sl][:, 0:t.n], rS(sl, 0, t.n)
            it = 0
            for J in range(4):
                for bi, (nm, abuf, aname) in enumerate((("c", aC, "aC"), ("p", aP, "aP"), ("a", aA, "aA"))):
                    rw, rm = ws_take("wb%s%d" % (nm, J)), ws_take("m%s%d" % (nm, J))
                    for jj in range(2):
                        j = 2 * J + jj
                        cs = slice(jj * 128, (jj + 1) * 128)
                        for t in tiles:
                            n = t.n
                            p = it % 2
                            it += 1
                            S_sg, S_tmp = 3 + p, 9 + p
                            b_br, b_m = balloc(), balloc()
                            mm(banks[b_br][:, 0:n], [(wring[:, rw, kc, cs], abuf[:, kc, t.col:t.col + n]) for kc in range(KC)],
                               [rA(aname, kc, t.col, n) for kc in range(KC)] + [rW(rw)], [rB(b_br, 0, n)], name="mm_br")
                            mm(banks[b_m][:, 0:n], [(wring[:, rm, kc, cs], hT[:, kc, t.col:t.col + n]) for kc in range(KC)],
                               [rA("hT", kc, t.col, n) for kc in range(KC)] + [rW(rm)], [rB(b_m, 0, n)], name="mm_m")
                            act(sview(S_sg, n), banks[b_m][:, 0:n], AF.Sigmoid, [rB(b_m, 0, n)], [rS(S_sg, 0, n)], name="sig")
                            aap, ares = acc(jj, t)
                            if bi == 0:
                                ew("dve", "tt", [rB(b_br, 0, n), rS(S_sg, 0, n)], [ares], out=aap, in0=banks[b_br][:, 0:n],
                                   in1=sview(S_sg, n), op=ALU.mult, name="mg0")
                            else:
                                ew("dve", "tt", [rB(b_br, 0, n), rS(S_sg, 0, n)], [rS(S_tmp, 0, n)], out=sview(S_tmp, n),
                                   in0=banks[b_br][:, 0:n], in1=sview(S_sg, n), op=ALU.mult, name="mgt")
                                if bi == 1:
                                    ew("dve", "tt", [rS(S_tmp, 0, n), ares], [ares], out=aap, in0=sview(S_tmp, n), in1=aap,
                                       op=ALU.add, name="mg1")
                                else:
                                    ew("dve", "tt", [rS(S_tmp, 0, n), ares], [rA("mgR", j, t.col, n)], out=mg[:, j, t.col:t.col + n],
                                       in0=sview(S_tmp, n), in1=aap, op=ALU.add, name="mg2")
                            bfree(b_br, b_m)
                    ws_release(2)

        def out_phase(l, half):
            tiles = HALVES[half]
            kvq = kv_steps(1) if (half == 0 and l == 0) else []
            for J in range(4):
                r, ri = ws_take("wo%d" % J, want_idx=True)
                for jj in range(2):
                    j = 2 * J + jj
                    cs = slice(jj * 128, (jj + 1) * 128)
                    for t in tiles:
                        n = t.n
                        b = balloc()
                        mm(banks[b][:, 0:n], [(wring[:, r, kc, cs], mg[:, kc, t.col:t.col + n]) for kc in range(KC)],
                           [rA("mgR", kc, t.col, n) for kc in range(KC)] + [rW(r)], [rB(b, 0, n)], name="mm_out")
                        xv = xT[:, j, t.col:t.col + n]
                        ew("dve", "tt", [rB(b, 0, n), rX(j, t.col, n)], [rX(j, t.col, n)], out=xv, in0=banks[b][:, 0:n], in1=xv,
                           op=ALU.add, name="resid")
                        bfree(b)
                        if t.kind == "P" and presq_ok(l, half):
                            ti = tiles.index(t)
                            si, off = 4 * (ti % 2) + j // 2, (j % 2) * 512
                            act(sview(si, n, BF16, off=off), xv, AF.Square, [rX(j, t.col, n)], [rSb(si, off, off + n)], name="sq_pre")
                        if t.kind == "P" and kvq:
                            kvq.pop(0)()
                ws_release_idx(ri)
            while kvq:
                kvq.pop(0)()

        def final_phase(half):
            tl = HALVES[half]
            for ti, t in enumerate(tl):
                p = ti % 2
                sq = [4 * p + k for k in range(4)]
                rms_stats(lambda c: xT[:, c, t.col:t.col + t.n], t.n, sq, 8 + ti, lambda c: rX(c, t.col, t.n),
                          presq=(t.kind == "P" and presq_ok(1, half)))
            par = 0
            for ti, t in enumerate(tl):
                rs = 8 + ti
                for c in range(KC):
                    xv = xT[:, c, t.col:t.col + t.n]
                    ew("dve", "stt", [rX(c, t.col, t.n), rS(rs, 0, t.n), "vt"], [rX(c, t.col, t.n)],
                       out=xv, in0=xv, scalar=vcol(V_FG, c), in1=sview(rs, t.n), op0=ALU.mult, op1=ALU.mult, name="yT")
                nsub = t.n // 128 if t.kind == "P" else 1
                ntok = 128 if t.kind == "P" else 64
                for s in range(nsub):
                    c0 = t.col + s * ntok
                    dst = y_p[t.row0 + s * 128: t.row0 + (s + 1) * 128, :] if t.kind == "P" else y_s[:, :]
                    store_rows(lambda c: xT[:, c, c0:c0 + ntok], ntok, dst, lambda c: rX(c, c0, ntok), 11 + 2 * par)
                    par ^= 1

        phases = [("setup", 0, setup)]
        for half in range(2):
            phases.append(("load_x%d" % half, 0, lambda half=half: load_x(half)))
            for l in range(2):
                if half == 0 and l == 0:
                    phases.append(("kv%d" % l, l, lambda l=l: kv_phase(l)))
                phases.append(("norm", l, lambda l=l, half=half: norm_phase(l, half)))
                phases.append(("conv", l, lambda l=l, half=half: conv_phase(l, half)))
                phases.append(("pool", l, lambda l=l, half=half: pool_phase(l, half)))
                phases.append(("att", l, lambda l=l, half=half: att_phase(l, half)))
                phases.append(("merge", l, lambda l=l, half=half: merge_phase(l, half)))
                phases.append(("out", l, lambda l=l, half=half: out_phase(l, half)))
            phases.append(("final%d" % half, 0, lambda half=half: final_phase(half)))
        for pi, (pname, pl, pf) in enumerate(phases):
            if STOP_AFTER is not None and pi >= STOP_AFTER:
                break
            cur["l"] = pl
            pf()
        if dry:
            return dry_order
        if STOP_AFTER is None:
            assert ws["taken"] == len(sched), (ws["taken"], len(sched))

        all_keys = sorted(T.dma_cnt.keys())
        T.finalize()

        sems = {e: es.enter_context(nc.semaphore("sem_" + e)) for e in ("pe", "act", "dve", "pool")}
        dma_sems = {k: es.enter_context(nc.semaphore("dsem_" + str(k))) for k in all_keys}
        block = es.enter_context(nc.Block())

        @block.tensor
        def _(e):
            T.emit("pe", e, sems, dma_sems)

        @block.scalar
        def _(e):
            T.emit("act", e, sems, dma_sems)

        @block.vector
        def _(e):
            T.emit("dve", e, sems, dma_sems)

        @block.gpsimd
        def _(e):
            T.emit("pool", e, sems, dma_sems)

        @block.sync
        def _(e):
            T.emit("sp", e, sems, dma_sems)
            for k in all_keys:
                e.wait_ge(dma_sems[k], 16 * T.dma_cnt[k])
    return nc


_CACHE = {}


def _pack_vecs(norm_g, conv_w, pool_scale, mem_norm_g, final_norm_g):
    v = np.zeros((128, NV), np.float32)

    def lay(a):
        a = np.asarray(a, np.float32)
        lead = a.shape[:-1]
        return np.moveaxis(a.reshape(lead + (8, 128)), -1, 0).reshape(128, -1)
    v[:, V_NG:V_NG + 16] = lay(norm_g)
    v[:, V_CW:V_CW + 48] = lay(conv_w)
    v[:, V_PS:V_PS + 16] = lay(pool_scale)
    v[:, V_MG:V_MG + 16] = lay(mem_norm_g)
    v[:, V_FG:V_FG + 8] = lay(final_norm_g)
    rc = np.zeros((4, 15), np.float32)
    for g, w in enumerate(WIN):
        rc[g] = 1.0 / np.minimum(np.arange(15) + 1, w)
    v[:, V_RC:V_RC + 60] = rc.reshape(1, 60)
    return v


def kernel(x_prompt, x_sample, mem_prompt, cache_mem_k, cache_mem_v, state_conv, state_pool,
           norm_g, w_in, conv_w, pool_w, pool_scale, mem_norm_g, w_mem_kv,
           w_br_conv, w_br_pool, w_br_att, w_out, final_norm_g):
    f = lambda a: np.ascontiguousarray(np.asarray(a), dtype=np.float32)
    if "nc" not in _CACHE:
        _CACHE["nc"] = build_program(build_program(None))
    nc = _CACHE["nc"]
    vecs = _pack_vecs(norm_g, conv_w, pool_scale, mem_norm_g, final_norm_g)
    ident = np.eye(128, dtype=np.float32)
    shared = {"w_in": f(w_in), "pool_w": f(pool_w), "w_kv": f(w_mem_kv), "w_bc": f(w_br_conv), "w_bp": f(w_br_pool),
              "w_ba": f(w_br_att), "w_o": f(w_out), "vecs": vecs, "ident": ident}
    x_prompt, x_sample, mem_prompt = f(x_prompt), f(x_sample), f(mem_prompt)
    cache_mem_k, cache_mem_v, state_conv, state_pool = f(cache_mem_k), f(cache_mem_v), f(state_conv), f(state_pool)
    in_maps = []
    for c in range(8):
        bs = slice(16 * c, 16 * c + 16)
        m = dict(shared)
        m["x_p"] = x_prompt[c]
        m["x_s"] = f(x_sample[bs].transpose(1, 0, 2).reshape(64, D))
        m["mem"] = mem_prompt[c]
        m["ck"] = f(cache_mem_k[:, bs])
        m["cv"] = f(cache_mem_v[:, bs])
        m["st_c"] = f(state_conv[:, bs].transpose(0, 2, 1, 3).reshape(2, 32, D))
        m["st_p"] = f(state_pool[:, bs].transpose(0, 2, 1, 3).reshape(2, 240, D))
        in_maps.append(m)
    res = run_bass_kernel_spmd(nc, in_maps, core_ids=list(range(8)))
    R = res.results
    y_prompt = np.stack([R[c]["y_p"] for c in range(8)], 0)
    y_sample = np.concatenate([R[c]["y_s"].reshape(4, 16, D).transpose(1, 0, 2) for c in range(8)], 0)
    mk = np.stack([R[c]["o_mk"].reshape(2, NMEM, 4, 256) for c in range(8)], 1)
    mv = np.stack([R[c]["o_mv"].reshape(2, NMEM, 4, 256) for c in range(8)], 1)
    cvp = np.stack([R[c]["o_cvp"] for c in range(8)], 1)
    plp = np.stack([R[c]["o_plp"] for c in range(8)], 1)
    cvs = np.concatenate([R[c]["o_cvs"].reshape(2, 2, 16, D).transpose(0, 2, 1, 3) for c in range(8)], 1)
    pls = np.concatenate([R[c]["o_pls"].reshape(2, 15, 16, D).transpose(0, 2, 1, 3) for c in range(8)], 1)
    out = (y_prompt, y_sample, mk, mv, cvp, plp, cvs, pls)
    return tuple(np.ascontiguousarray(o, dtype=np.float32) for o in out)
```
